# Optimizing a Trainium2 kernel written in Bass

```python
import math
import jax, jax.numpy as jnp
from jax import lax
import numpy as np


D_MODEL = 1024
BATCH = 8
SEQ = 4096
DEPTH = 1
DEC_BATCH = 4
DEC_SEQ = 4096
PAST_LEN = 128

D_HYENA = D_MODEL // 2
HYENA_ORDER = 2
SHORT_CONV = 3
FILTER_EMB = 33
FILTER_ORDER = 64
N_FILTER_INNER = 2
N_DIRECTIONS = 2
N_FILT_CH = N_DIRECTIONS * HYENA_ORDER * D_HYENA
FAST_DECAY_PCT = 0.3
SLOW_DECAY_PCT = 1.5
DECAY_TARGET = 1e-2
FILTER_OUT_SCALE = 0.05
N_Q_HEADS = 8
N_KV_HEADS = 2
HEAD_DIM = 64
D_ATTN = N_Q_HEADS * HEAD_DIM
D_KV = N_KV_HEADS * HEAD_DIM
Q_PER_KV = N_Q_HEADS // N_KV_HEADS
WINDOW = 128
BLOCK = 128
ROPE_THETA = 500000.0
ROT_DIM = HEAD_DIM // 4
D_IN = (HYENA_ORDER + 1) * D_HYENA + D_ATTN + 2 * D_KV + 2 * D_MODEL
D_FF = (8 * D_MODEL + 3 * 256 - 1) // (3 * 256) * 256
EPS = 1e-6
NEG_INF = -1e30

kernel_name = 'hybrid_hyena_swa_gated_encoder'


def rmsnorm(x, g):
    xf = x.astype(jnp.float32)
    y = xf * lax.rsqrt(jnp.mean(xf * xf, axis=-1, keepdims=True) + EPS)
    return (y * g.astype(jnp.float32)).astype(x.dtype)


def hyena_filters(L, w0, b0, w_inner, b_inner, w_out, freq):
    f32 = jnp.float32
    bands = (FILTER_EMB - 1) // 2
    t = jnp.linspace(0.0, 1.0, L, dtype=f32)[:, None]
    w = 2.0 * math.pi * jnp.arange(L, dtype=f32)[:, None] / L
    f = jnp.linspace(1e-4, bands - 1, bands, dtype=f32)[None, :]
    z = jnp.concatenate([t, jnp.cos(f * w), jnp.sin(f * w)], axis=-1)
    fr = freq.astype(f32)
    h = jnp.sin(fr * (z @ w0.astype(f32) + b0.astype(f32)))
    for i in range(N_FILTER_INNER):
        h = jnp.sin(fr * (h @ w_inner[i].astype(f32) + b_inner[i].astype(f32)))
    h = h @ w_out.astype(f32)
    min_decay = math.log(DECAY_TARGET) / SLOW_DECAY_PCT
    max_decay = math.log(DECAY_TARGET) / FAST_DECAY_PCT
    deltas = jnp.tile(jnp.linspace(min_decay, max_decay, D_HYENA, dtype=f32), N_DIRECTIONS * HYENA_ORDER)
    h = h * jnp.exp(-t * jnp.abs(deltas))
    return h.reshape(L, N_DIRECTIONS, HYENA_ORDER, D_HYENA)


def bidir_fftconv(u, h_fwd, h_bwd):
    L = u.shape[1]
    k = jnp.concatenate([h_fwd, jnp.zeros_like(h_fwd[:1]), h_bwd[:0:-1]], axis=0)
    k_f = jnp.fft.rfft(k, n=2 * L, axis=0)
    u_f = jnp.fft.rfft(u, n=2 * L, axis=1)
    return jnp.fft.irfft(u_f * k_f[None], n=2 * L, axis=1)[:, :L]


def hyena_mixer(u, short_w, short_b, filt, hyena_bias):
    L = u.shape[1]
    pad = SHORT_CONV // 2
    up = jnp.pad(u, ((0, 0), (pad, pad), (0, 0)))
    u = sum(up[:, j:j + L] * short_w[j] for j in range(SHORT_CONV)) + short_b
    v, x1, x2 = jnp.split(u, HYENA_ORDER + 1, axis=-1)
    z = v.astype(jnp.float32)
    for o, gate in enumerate((x1, x2)):
        z = gate.astype(jnp.float32) * (bidir_fftconv(z, filt[:, 0, o], filt[:, 1, o])
                                        + hyena_bias[o].astype(jnp.float32) * z)
    return z.astype(u.dtype)


def partial_rope(x, pos):
    half = ROT_DIM // 2
    inv = 1.0 / (ROPE_THETA ** (jnp.arange(0, ROT_DIM, 2, dtype=jnp.float32) / ROT_DIM))
    ang = pos[:, None] * inv[None, :]
    cos = jnp.cos(ang)[None, :, None, :]
    sin = jnp.sin(ang)[None, :, None, :]
    xr = x[..., :ROT_DIM].astype(jnp.float32)
    a, b = xr[..., :half], xr[..., half:]
    rot = jnp.concatenate([a * cos - b * sin, b * cos + a * sin], axis=-1)
    return jnp.concatenate([rot.astype(x.dtype), x[..., ROT_DIM:]], axis=-1)


def window_attention(q, k, v, sink):
    B, L = q.shape[0], q.shape[1]
    nb = L // BLOCK
    qb = q.reshape(B, nb, BLOCK, N_KV_HEADS, Q_PER_KV, HEAD_DIM)

    def band(t):
        tp = jnp.pad(t, ((0, 0), (BLOCK, BLOCK), (0, 0), (0, 0))).reshape(B, nb + 2, BLOCK, N_KV_HEADS, HEAD_DIM)
        return jnp.concatenate([tp[:, 0:nb], tp[:, 1:nb + 1], tp[:, 2:nb + 2]], axis=2)

    kb, vb = band(k), band(v)
    s = jnp.einsum('bnqhgd,bnkhd->bnhgqk', qb, kb).astype(jnp.float32) * (HEAD_DIM ** -0.5)
    qpos = jnp.arange(nb)[:, None] * BLOCK + jnp.arange(BLOCK)[None, :]
    kpos = jnp.arange(nb)[:, None] * BLOCK - BLOCK + jnp.arange(3 * BLOCK)[None, :]
    valid = ((jnp.abs(qpos[:, :, None] - kpos[:, None, :]) <= WINDOW)
             & (kpos[:, None, :] >= 0) & (kpos[:, None, :] < L))
    s = jnp.where(valid[None, :, None, None], s, NEG_INF)
    sk = sink.astype(jnp.float32).reshape(1, 1, N_KV_HEADS, Q_PER_KV, 1, 1)
    m = jnp.maximum(jnp.max(s, axis=-1, keepdims=True), sk)
    p = jnp.exp(s - m)
    p = p / (jnp.sum(p, axis=-1, keepdims=True) + jnp.exp(sk - m))
    o = jnp.einsum('bnhgqk,bnkhd->bnqhgd', p.astype(v.dtype), vb)
    return o.reshape(B, L, D_ATTN)


def encoder_layer(x, norm1_g, w_in, short_w, short_b, filt_w0, filt_b0, filt_w_inner, filt_b_inner,
                  filt_w_out, filt_freq, hyena_bias, sink_logit, w_up_hyena, w_up_attn, w_out,
                  norm2_g, w_ff_gate, w_ff_up, w_ff_down):
    B, L, _ = x.shape
    h = rmsnorm(x, norm1_g)
    proj = h @ w_in
    splits = np.cumsum([(HYENA_ORDER + 1) * D_HYENA, D_ATTN, D_KV, D_KV, D_MODEL]).tolist()
    u_h, q, k, v, g_h, g_a = jnp.split(proj, splits, axis=-1)
    filt = hyena_filters(L, filt_w0, filt_b0, filt_w_inner, filt_b_inner, filt_w_out, filt_freq)
    y_h = hyena_mixer(u_h, short_w, short_b, filt, hyena_bias)
    pos = jnp.arange(L, dtype=jnp.float32)
    q = partial_rope(q.reshape(B, L, N_Q_HEADS, HEAD_DIM), pos)
    k = partial_rope(k.reshape(B, L, N_KV_HEADS, HEAD_DIM), pos)
    y_a = window_attention(q, k, v.reshape(B, L, N_KV_HEADS, HEAD_DIM), sink_logit)
    merged = jax.nn.sigmoid(g_h) * (y_h @ w_up_hyena) + jax.nn.sigmoid(g_a) * (y_a @ w_up_attn)
    x = x + merged @ w_out
    h = rmsnorm(x, norm2_g)
    x = x + (jax.nn.silu(h @ w_ff_gate) * (h @ w_ff_up)) @ w_ff_down
    return x


def encoder_trunk(x, norm1_g, w_in, short_w, short_b, filt_w0, filt_b0, filt_w_inner, filt_b_inner,
                  filt_w_out, filt_freq, hyena_bias, sink_logit, w_up_hyena, w_up_attn, w_out,
                  norm2_g, w_ff_gate, w_ff_up, w_ff_down, final_g):
    for l in range(DEPTH):
        x = encoder_layer(x, norm1_g[l], w_in[l], short_w[l], short_b[l], filt_w0[l], filt_b0[l],
                          filt_w_inner[l], filt_b_inner[l], filt_w_out[l], filt_freq[l], hyena_bias[l],
                          sink_logit[l], w_up_hyena[l], w_up_attn[l], w_out[l], norm2_g[l],
                          w_ff_gate[l], w_ff_up[l], w_ff_down[l])
    return rmsnorm(x, final_g)


def setup_inputs(seed: int = 0) -> dict:
    key = jax.random.key(seed)
    ks = jax.random.split(key, 24)
    f32 = jnp.float32

    def nrm(k, shape, scale):
        return jax.random.normal(k, shape, f32) * scale

    D3 = (HYENA_ORDER + 1) * D_HYENA
    return {
        'x_prompt': nrm(ks[0], (BATCH, SEQ, D_MODEL), 1.0),
        'x_sample': nrm(ks[1], (DEC_BATCH, DEC_SEQ, D_MODEL), 1.0),
        'norm1_g': 1.0 + nrm(ks[2], (DEPTH, D_MODEL), 0.01),
        'w_in': nrm(ks[3], (DEPTH, D_MODEL, D_IN), D_MODEL ** -0.5),
        'short_w': nrm(ks[4], (DEPTH, SHORT_CONV, D3), SHORT_CONV ** -0.5),
        'short_b': nrm(ks[5], (DEPTH, D3), 0.01),
        'filt_w0': nrm(ks[6], (DEPTH, FILTER_EMB, FILTER_ORDER), FILTER_EMB ** -0.5),
        'filt_b0': nrm(ks[7], (DEPTH, FILTER_ORDER), 0.1),
        'filt_w_inner': nrm(ks[8], (DEPTH, N_FILTER_INNER, FILTER_ORDER, FILTER_ORDER), FILTER_ORDER ** -0.5),
        'filt_b_inner': nrm(ks[9], (DEPTH, N_FILTER_INNER, FILTER_ORDER), 0.1),
        'filt_w_out': nrm(ks[10], (DEPTH, FILTER_ORDER, N_FILT_CH), FILTER_OUT_SCALE * FILTER_ORDER ** -0.5),
        'filt_freq': 1.0 + nrm(ks[11], (DEPTH, FILTER_ORDER), 0.01),
        'hyena_bias': nrm(ks[12], (DEPTH, HYENA_ORDER, D_HYENA), 1.0),
        'sink_logit': nrm(ks[13], (DEPTH, N_Q_HEADS), 0.5),
        'w_up_hyena': nrm(ks[14], (DEPTH, D_HYENA, D_MODEL), D_HYENA ** -0.5),
        'w_up_attn': nrm(ks[15], (DEPTH, D_ATTN, D_MODEL), D_ATTN ** -0.5),
        'w_out': nrm(ks[16], (DEPTH, D_MODEL, D_MODEL), D_MODEL ** -0.5),
        'norm2_g': 1.0 + nrm(ks[17], (DEPTH, D_MODEL), 0.01),
        'w_ff_gate': nrm(ks[18], (DEPTH, D_MODEL, D_FF), D_MODEL ** -0.5),
        'w_ff_up': nrm(ks[19], (DEPTH, D_MODEL, D_FF), D_MODEL ** -0.5),
        'w_ff_down': nrm(ks[20], (DEPTH, D_FF, D_MODEL), D_FF ** -0.5),
        'final_g': 1.0 + nrm(ks[21], (D_MODEL,), 0.01),
    }


def reference(x_prompt, x_sample, norm1_g, w_in, short_w, short_b, filt_w0, filt_b0, filt_w_inner,
              filt_b_inner, filt_w_out, filt_freq, hyena_bias, sink_logit, w_up_hyena, w_up_attn, w_out,
              norm2_g, w_ff_gate, w_ff_up, w_ff_down, final_g):
    weights = (norm1_g, w_in, short_w, short_b, filt_w0, filt_b0, filt_w_inner, filt_b_inner,
               filt_w_out, filt_freq, hyena_bias, sink_logit, w_up_hyena, w_up_attn, w_out,
               norm2_g, w_ff_gate, w_ff_up, w_ff_down, final_g)
    y_prompt = encoder_trunk(x_prompt, *weights)
    y_sample = encoder_trunk(x_sample, *weights)
    return (y_prompt, y_sample)
```

```python
import math
from contextlib import ExitStack
import numpy as np
import ml_dtypes
import concourse.bass as bass
import concourse.mybir as mybir
from concourse.bass_utils import run_bass_kernel_spmd

F32 = mybir.dt.float32
BF16 = mybir.dt.bfloat16
AF = mybir.ActivationFunctionType
ALU = mybir.AluOpType
NPBF = ml_dtypes.bfloat16

D = 1024
L = 4096
NSEQ = 2
DH = 512
D_IN = 4352
D_FF = 2816
NFF = 22
NK1 = 65
CG = 64
NCG = 8
EPS = 1e-6
MAGIC = 12582912.0
TWO_PI = 2.0 * math.pi


class Tk:
    __slots__ = ("w", "r")

    def __init__(self):
        self.w = None
        self.r = {}


class Sched:
    def __init__(self, nc, es, ndma=64):
        self.nc = nc
        self.eng = {"pe": nc.tensor, "dve": nc.vector, "act": nc.scalar, "pool": nc.gpsimd, "sp": nc.sync}
        self.sem = {k: es.enter_context(nc.semaphore("s_" + k)) for k in self.eng}
        self.cnt = {k: 0 for k in self.eng}
        self.waited = {k: {} for k in self.eng}
        self.ndma = ndma
        for i in range(ndma):
            self.sem[("d", i)] = es.enter_context(nc.semaphore("s_d%d" % i))
            self.cnt[("d", i)] = 0
        self.dpool = {"sp": list(range(0, ndma - 16)), "pool": list(range(ndma - 16, ndma))}
        self.dnext = {"sp": 0, "pool": 0}
        self.ninstr = 0

    def _deps(self, reads, writes):
        deps = {}

        def add(k, v):
            if deps.get(k, 0) < v:
                deps[k] = v

        for t in reads:
            if t.w is not None:
                add(*t.w)
        for t in writes:
            if t.w is not None:
                add(*t.w)
            for k, v in t.r.items():
                add(k, v)
        return deps

    def _wait(self, e, deps):
        w = self.waited[e]
        for k, v in deps.items():
            if k == e and e == "pe":
                continue
            if w.get(k, 0) >= v:
                continue
            if isinstance(k, tuple):
                v = self.cnt[k]
            self.eng[e].wait_ge(self.sem[k], v)
            w[k] = v
            self.ninstr += 1

    def _mark(self, ev, reads, writes):
        for t in writes:
            t.w = ev
            t.r = {}
        for t in reads:
            k, v = ev
            if t.r.get(k, 0) < v:
                t.r[k] = v

    def op(self, e, fn, reads=(), writes=(), sig=True, waits=()):
        self._wait(e, self._deps(reads, list(writes) + list(waits)))
        ins = fn(self.eng[e])
        self.ninstr += 1
        if sig:
            self.cnt[e] += 1
            ins.then_inc(self.sem[e], 1)
            ev = (e, self.cnt[e])
        else:
            ev = (e, self.cnt[e] + 1)
        self._mark(ev, reads, writes)

    def dma(self, q, out, in_, reads=(), writes=()):
        self._wait(q, self._deps(reads, writes))
        pl = self.dpool[q]
        i = pl[self.dnext[q] % len(pl)]
        self.dnext[q] += 1
        k = ("d", i)
        self.cnt[k] += 16
        self.eng[q].dma_start(out=out, in_=in_).then_inc(self.sem[k], 16)
        self.ninstr += 1
        self._mark((k, self.cnt[k]), reads, writes)

    def barrier(self, engines=("pe", "dve", "act", "pool", "sp")):
        deps = {k: v for k, v in self.cnt.items() if v > 0}
        for e in engines:
            self._wait(e, dict(deps))


def _consts():
    c = {}
    c["ident"] = np.eye(128, dtype=np.float32).astype(NPBF)
    inv = (1.0 / (500000.0 ** (np.arange(0, 16, 2, dtype=np.float32) / 16.0))).astype(np.float32)
    pos = np.arange(L, dtype=np.float32)
    ang = (pos[:, None] * inv[None, :]).astype(np.float32)
    cs = np.stack([np.cos(ang), np.sin(ang)], 1).astype(np.float32)
    c["rope"] = np.ascontiguousarray(cs.reshape(32, 128, 2, 8).transpose(1, 0, 2, 3))
    j = np.arange(128)[:, None]
    i = np.arange(128)[None, :]
    neg = np.stack([np.where(j >= i, 0.0, -30000.0), np.where(j <= i, 0.0, -30000.0)], 1)
    c["negm"] = np.ascontiguousarray(np.broadcast_to(neg[:, :, None, :], (128, 2, 4, 128))).astype(np.float32).astype(NPBF)
    n1 = np.arange(128, dtype=np.float64)
    k1 = np.arange(NK1, dtype=np.float64)
    th = TWO_PI * n1[:, None] * k1[None, :] / 128.0
    c["fw1"] = np.stack([np.cos(th), -np.sin(th)], 1).astype(np.float32).astype(NPBF)
    n2 = np.arange(64, dtype=np.float64)
    k2 = np.arange(64, dtype=np.float64)
    th = TWO_PI * (n2[:, None, None] * k2[None, None, :] / 64.0 + n2[:, None, None] * k1[None, :, None] / 8192.0)
    mr, mi = np.cos(th), -np.sin(th)
    c["fm2"] = np.stack([np.concatenate([mr, mi], -1), np.concatenate([-mi, mr], -1)], 2).astype(np.float32).astype(NPBF)
    ph = TWO_PI * k2[:, None] * n2[None, :] / 64.0
    vr, vi = np.cos(ph), np.sin(ph)
    fv0 = np.concatenate([vr, vi], 1)
    fv1 = np.concatenate([-vi, vr], 1)
    c["fv"] = np.stack([np.concatenate([fv0, -fv0], 0), np.concatenate([fv1, fv1], 0)], 1).astype(np.float32).astype(NPBF)
    n1h = np.arange(64, dtype=np.float64)
    ps = TWO_PI * (k1[:, None, None] * n1h[None, None, :] / 128.0 + k1[:, None, None] * n2[None, :, None] / 8192.0)
    al = np.full(NK1, 2.0)
    al[0] = 1.0
    al[64] = 1.0
    sc = (al / 8192.0)[:, None, None]
    c["fe"] = np.stack([sc * np.cos(ps), -sc * np.sin(ps)], 2).astype(np.float32).astype(NPBF)
    n = np.arange(8192)
    m = np.where(n < L, n, 8192 - n)
    m[L] = 0
    t = np.linspace(0.0, 1.0, L, dtype=np.float32)
    w = (2.0 * math.pi * np.arange(L, dtype=np.float32) / L).astype(np.float32)
    f = np.linspace(1e-4, 15, 16, dtype=np.float32)
    fw = (f[None, :] * w[:, None]).astype(np.float32)
    z = np.concatenate([t[:, None], np.cos(fw), np.sin(fw)], -1).astype(np.float32)
    c["zt"] = np.ascontiguousarray(z[m].T)
    negt = -t[m]
    negt[L] = -1e4
    c["negt"] = np.ascontiguousarray(negt.reshape(128, 64).astype(np.float32))
    mind = math.log(1e-2) / 1.5
    maxd = math.log(1e-2) / 0.3
    dl = np.abs(np.linspace(mind, maxd, 512, dtype=np.float32))
    c["absd"] = np.ascontiguousarray(np.tile(dl[None, :], (128, 1)).astype(np.float32))
    return c


CONST_SPECS = [("ident", [128, 128], BF16), ("rope", [128, 32, 2, 8], F32), ("negm", [128, 2, 4, 128], BF16),
               ("fw1", [128, 2, NK1], BF16), ("fm2", [64, NK1, 2, 128], BF16), ("fv", [128, 2, 128], BF16),
               ("fe", [NK1, 64, 2, 64], BF16), ("zt", [33, 8192], F32), ("negt", [128, 64], F32),
               ("absd", [128, 512], F32)]

IN_SPECS = [("x", [NSEQ, L, D]), ("w_in", [D, D_IN]), ("g1r", [128, D]), ("shortwp", [128, 12, 3]),
            ("shortbp", [128, 12]), ("fw0", [33, 64]), ("fb0", [64, 1]), ("fwi", [2, 64, 64]), ("fbi", [64, 2]),
            ("ffreq", [64, 1]), ("fwo", [64, 2048]), ("hbias", [1, 1024]), ("sinkr", [128, 8]),
            ("w_up_h", [DH, D]), ("w_up_a", [DH, D]), ("w_o", [D, D]), ("g2r", [128, D]),
            ("w_gate", [D, D_FF]), ("w_up", [D, D_FF]), ("w_down", [D_FF, D]), ("gfr", [128, D])]


def build(debug=False):
    SK = "ExternalOutput" if debug else "Internal"
    nc = bass.Bass("TRN2", target_bir_lowering=False)
    es = ExitStack()
    S = Sched(nc, es)
    I = {}
    for name, shp in IN_SPECS:
        I[name] = nc.dram_tensor(name, list(shp), F32, kind="ExternalInput").ap()
    for name, shp, dt in CONST_SPECS:
        I[name] = nc.dram_tensor(name, list(shp), dt, kind="ExternalInput").ap()
    yout = nc.dram_tensor("y", [NSEQ, L, D], F32, kind="ExternalOutput").ap()
    PROJ = nc.dram_tensor("proj_s", [NSEQ, L, 2304], BF16, kind=SK).ap()
    SG = nc.dram_tensor("sg_s", [NSEQ, 2048, L], BF16, kind=SK).ap()
    YAT = nc.dram_tensor("yat_s", [NSEQ, DH, L], BF16, kind=SK).ap()
    YHT = nc.dram_tensor("yht_s", [NSEQ, DH, L], BF16, kind=SK).ap()
    X1 = nc.dram_tensor("x1_s", [NSEQ * L, D], F32, kind=SK).ap()
    KH = nc.dram_tensor("kh_s", [2, NCG, 128, NK1 * CG], BF16, kind=SK).ap()
    tk_proj, tk_sg, tk_yat, tk_yht, tk_x1, tk_kh = Tk(), Tk(), Tk(), Tk(), Tk(), Tk()
    tk_in = Tk()
    tk_out = Tk()

    uid = {"n": 0}

    def sb(st, name, shape, dt):
        uid["n"] += 1
        return st.enter_context(nc.sbuf_tensor("sb%d_%s" % (uid["n"], name), list(shape), dt))

    banks = []
    for b in range(8):
        banks.append((es.enter_context(nc.psum_tensor("ps%d" % b, [128, 512], F32)), Tk()))
    bstate = {"i": 0}

    def psum(pool=None):
        if pool is None:
            b = banks[(0, 1, 2, 3, 5, 6, 7)[bstate["i"] % 7]]
            bstate["i"] += 1
            return b
        lo, n = pool
        k = "p%d_%d" % (lo, n)
        bstate[k] = bstate.get(k, -1) + 1
        return banks[lo + bstate[k] % n]

    class Ring:
        def __init__(self, st, name, shape, dt, n):
            self.t = [(sb(st, "%s%d" % (name, i), shape, dt), Tk()) for i in range(n)]
            self.i = 0

        def next(self):
            r = self.t[self.i % len(self.t)]
            self.i += 1
            return r

    ident = sb(es, "ident", [128, 128], BF16)
    tk_c = Tk()
    S.dma("sp", ident[:], I["ident"][:, :], reads=[tk_in], writes=[tk_c])

    def mm(out, lhsT, rhs, start, stop, reads, writes, sig):
        S.op("pe", lambda e: e.matmul(out, lhsT, rhs, start=start, stop=stop), reads=reads, writes=writes, sig=sig)

    def norm_load(st_ring_x, srcs, q="sp"):
        xts = [st_ring_x.next() for _ in range(len(srcs))]
        for i in range(len(srcs)):
            S.dma(q, xts[i][0][:], srcs[i], reads=[tk_in, tk_x1], writes=[xts[i][1]])
        return xts

    def norm_group_to_T(st_ring_x, xn_ring, small_ring, srcs, dstT, dstT_tk, col0s, junk, junk_tk, grow, grow_tk, xts=None,
                        phase="both", xns=None):
        n = len(srcs)
        if phase == "trans":
            return norm_trans(xns, dstT, dstT_tk, col0s)
        if xts is None:
            xts = norm_load(st_ring_x, srcs)
        sms = [small_ring.next() for _ in range(n)]
        xns = [xn_ring.next() for _ in range(n)]
        for i in range(n):
            (xt, xtk), (sm, smtk) = xts[i], sms[i]
            S.op("act", lambda e: e.activation(out=junk[:], in_=xt[:], func=AF.Square, accum_out=sm[:, 0:1]),
                 reads=[xtk], writes=[junk_tk, smtk])
        for i in range(n):
            sm, smtk = sms[i]
            S.op("dve", lambda e: e.tensor_scalar(sm[:, 1:2], sm[:, 0:1], 1.0 / D, EPS, op0=ALU.mult, op1=ALU.add),
                 reads=[smtk], writes=[smtk])
        for i in range(n):
            sm, smtk = sms[i]
            S.op("act", lambda e: e.activation(out=sm[:, 2:3], in_=sm[:, 1:2], func=AF.Sqrt), reads=[smtk], writes=[smtk])
        for i in range(n):
            sm, smtk = sms[i]
            S.op("dve", lambda e: e.reciprocal(sm[:, 3:4], sm[:, 2:3]), reads=[smtk], writes=[smtk])
        for i in range(n):
            (xt, xtk), (sm, smtk), (xn, xntk) = xts[i], sms[i], xns[i]
            S.op("dve", lambda e: e.scalar_tensor_tensor(out=xn[:], in0=xt[:], scalar=sm[:, 3:4], in1=grow[:], op0=ALU.mult,
                                                         op1=ALU.mult), reads=[xtk, smtk, grow_tk], writes=[xntk])
        if phase == "stats":
            return xts, xns
        norm_trans(xns, dstT, dstT_tk, col0s)
        return xts

    def norm_trans(xns, dstT, dstT_tk, col0s):
        for i in range(len(xns)):
            xn, xntk = xns[i]
            bank, btk = psum()
            bv = bank[:].bitcast(BF16)
            for kc in range(8):
                S.op("pe", lambda e: e.transpose(bv[:, kc * 128:(kc + 1) * 128], xn[:, kc * 128:(kc + 1) * 128], ident[:]),
                     reads=[xntk, tk_c], writes=[btk], sig=(kc == 7))
            if i % 2 == 0:
                S.op("act", lambda e: e.copy(dstT[:, :, col0s[i]:col0s[i] + 128], bv.rearrange("p (k t) -> p k t", k=8)),
                     reads=[btk], writes=[dstT_tk])
            else:
                S.op("dve", lambda e: e.tensor_copy(dstT[:, :, col0s[i]:col0s[i] + 128], bv.rearrange("p (k t) -> p k t", k=8)),
                     reads=[btk], writes=[dstT_tk])

    FC = {}
    tk_f = Tk()

    def load_fft_consts(st):
        FC["fw1"] = sb(st, "fw1", [128, 2 * NK1], BF16)
        FC["fm2"] = sb(st, "fm2", [64, NK1, 2, 128], BF16)
        FC["fv"] = sb(st, "fv", [128, 2, 128], BF16)
        FC["fe"] = sb(st, "fe", [NK1, 64, 2, 64], BF16)
        S.dma("sp", FC["fw1"][:], I["fw1"].rearrange("p a k -> p (a k)"), reads=[tk_in], writes=[tk_f])
        S.dma("sp", FC["fm2"][:], I["fm2"][:, :, :, :], reads=[tk_in], writes=[tk_f])
        S.dma("sp", FC["fv"][:], I["fv"][:, :, :], reads=[tk_in], writes=[tk_f])
        S.dma("sp", FC["fe"][:], I["fe"][:, :, :, :], reads=[tk_in], writes=[tk_f])

    def fft_forward(st, xin, xin_tk, krows, YP, YP_tk, consume, alt=False, ywaits=(), part="both"):
        fw1, fm2 = FC["fw1"], FC["fm2"]
        Yv = YP[0:64, :].rearrange("p (q c) -> p q c", c=CG)
        for c0 in (range(0, CG, 3) if part in ("both", "s1") else ()):
            nch = min(3, CG - c0)
            bank, btk = psum()
            for cc in range(nch):
                mm(bank[0:64, cc * 130:(cc + 1) * 130], xin[0:krows, :, c0 + cc], fw1[0:krows, :], True, True,
                   [xin_tk, tk_f], [btk], cc == nch - 1)
            if alt and (c0 // 3) % 2 == 1:
                S.op("dve", lambda e: e.tensor_copy(Yv[:, :, c0:c0 + nch],
                                                    bank[0:64, 0:nch * 130].rearrange("p (c q) -> p q c", c=nch)),
                     reads=[btk], writes=[YP_tk], waits=ywaits)
            else:
                S.op("act", lambda e: e.copy(Yv[:, :, c0:c0 + nch],
                                             bank[0:64, 0:nch * 130].rearrange("p (c q) -> p q c", c=nch)),
                     reads=[btk], writes=[YP_tk], waits=ywaits)
        for k1a in (range(0, NK1, 8) if part in ("both", "s2") else ()):
            nk = min(8, NK1 - k1a)
            bank, btk = psum()
            for kk in range(nk):
                k1 = k1a + kk
                o_x = bank[:, kk * CG:(kk + 1) * CG]
                mm(o_x, fm2[:, k1, 0, :], Yv[:, k1, :], True, False, [YP_tk, tk_f], [btk], False)
                mm(o_x, fm2[:, k1, 1, :], Yv[:, NK1 + k1, :], False, True, [YP_tk, tk_f], [btk], kk == nk - 1)
            consume(bank, btk, k1a, nk)

    with ExitStack() as st:
        load_fft_consts(st)
        wo = sb(st, "fwo", [64, 2048], BF16)
        fpar = sb(st, "fpar", [64, 8], F32)
        negt = sb(st, "negt", [128, 64], F32)
        absd = sb(st, "absd", [128, 512], F32)
        hb = sb(st, "hb", [1, 1024], F32)
        h3 = sb(st, "h3", [64, 8192], BF16)
        st_mlp = ExitStack()
        zt = sb(st_mlp, "zt", [33, 8192], F32)
        w0 = sb(st_mlp, "fw0", [33, 64], F32)
        wi = sb(st_mlp, "fwi", [64, 2, 64], F32)
        tk_fp = Tk()
        tk_h3 = Tk()
        S.dma("sp", zt[:], I["zt"][:, :], reads=[tk_in], writes=[tk_fp])
        S.dma("sp", w0[:], I["fw0"][:, :], reads=[tk_in], writes=[tk_fp])
        S.dma("sp", wi[:], I["fwi"].rearrange("a i o -> i a o"), reads=[tk_in], writes=[tk_fp])
        S.dma("pool", wo[:], I["fwo"][:, :], reads=[tk_in], writes=[tk_fp])
        S.dma("sp", fpar[:, 0:1], I["ffreq"][:, :], reads=[tk_in], writes=[tk_fp])
        S.dma("sp", fpar[:, 1:2], I["fb0"][:, :], reads=[tk_in], writes=[tk_fp])
        S.dma("sp", fpar[:, 2:4], I["fbi"][:, :], reads=[tk_in], writes=[tk_fp])
        S.dma("sp", negt[:], I["negt"][:, :], reads=[tk_in], writes=[tk_fp])
        S.dma("sp", absd[:], I["absd"][:, :], reads=[tk_in], writes=[tk_fp])
        S.dma("sp", hb[:], I["hbias"][:, :], reads=[tk_in], writes=[tk_fp])
        S.op("dve", lambda e: e.tensor_scalar(fpar[:, 4:7], fpar[:, 1:4], fpar[:, 0:1], None, op0=ALU.mult),
             reads=[tk_fp], writes=[tk_fp])
        hring = Ring(st_mlp, "fh", [64, 512], F32, 8)
        tring = Ring(st_mlp, "ft", [64, 512], F32, 10)
        for pg in range(4):
            curs = [None] * 4
            for layer in range(3):
                for q4 in range(4):
                    pc = pg * 4 + q4
                    cur = curs[q4]
                    bank, btk = psum()
                    if layer == 0:
                        mm(bank[0:64, :], w0[:, :], zt[:, pc * 512:(pc + 1) * 512], True, True, [tk_fp], [btk], True)
                    else:
                        mm(bank[0:64, :], wi[:, layer - 1, :], cur[0][:], True, True, [tk_fp, cur[1]], [btk], True)
                    t1, t1k = tring.next()
                    S.op("dve", lambda e: e.tensor_scalar(t1[:], bank[0:64, :], fpar[:, 0:1], fpar[:, 4 + layer:5 + layer],
                                                          op0=ALU.mult, op1=ALU.add), reads=[btk, tk_fp], writes=[t1k])
                    t2, t2k = tring.next()
                    S.op("dve", lambda e: e.tensor_scalar(t2[:], t1[:], 1.0 / TWO_PI, MAGIC, op0=ALU.mult, op1=ALU.add),
                         reads=[t1k], writes=[t2k])
                    S.op("dve", lambda e: e.tensor_scalar(t2[:], t2[:], MAGIC, -TWO_PI, op0=ALU.subtract, op1=ALU.mult),
                         reads=[t2k], writes=[t2k])
                    S.op("dve", lambda e: e.tensor_tensor(t1[:], t1[:], t2[:], op=ALU.add), reads=[t1k, t2k], writes=[t1k])
                    if layer < 2:
                        hn, hnk = hring.next()
                        S.op("act", lambda e: e.activation(out=hn[:], in_=t1[:], func=AF.Sin), reads=[t1k], writes=[hnk])
                        curs[q4] = (hn, hnk)
                    else:
                        S.op("act", lambda e: e.activation(out=h3[:, pc * 512:(pc + 1) * 512], in_=t1[:], func=AF.Sin),
                             reads=[t1k], writes=[tk_h3])
        S.barrier()
        st_mlp.close()
        h3v = h3[:].rearrange("p (a b) -> p a b", b=64)
        kt = sb(st, "kt", [128, 64, 512], BF16)
        tk_kt = Tk()
        YPfs = [(sb(st, "ypf%d" % i, [128, 130 * CG], BF16), Tk()) for i in range(2)]
        dring = Ring(st, "fdec", [128, 512], F32, 2)
        xsring = Ring(st, "fxs", [128, 8 * CG], BF16, 3)
        for o in range(2):
            colf = (0 * 2 + o) * 512
            colb = (1 * 2 + o) * 512
            for n2 in range(64):
                dec, deck = dring.next()
                S.op("act", lambda e: e.activation(out=dec[:], in_=absd[:], func=AF.Exp, scale=negt[:, n2:n2 + 1]),
                     reads=[tk_fp], writes=[deck])
                bf_, bfk = psum()
                bb_, bbk = psum()
                mm(bf_[:, :], h3v[:, :, n2], wo[:, colf:colf + 512], True, True, [tk_h3, tk_fp], [bfk], True)
                mm(bb_[:, :], h3v[:, :, n2], wo[:, colb:colb + 512], True, True, [tk_h3, tk_fp], [bbk], True)
                S.op("dve", lambda e: e.tensor_tensor(kt[0:64, n2, :], bf_[0:64, :], dec[0:64, :], op=ALU.mult),
                     reads=[bfk, deck], writes=[tk_kt])
                S.op("dve", lambda e: e.tensor_tensor(kt[64:128, n2, :], bb_[64:128, :], dec[64:128, :], op=ALU.mult),
                     reads=[bbk, deck], writes=[tk_kt])
            S.op("dve", lambda e: e.tensor_tensor(kt[0:1, 0, :], kt[0:1, 0, :], hb[0:1, o * 512:(o + 1) * 512], op=ALU.add),
                 reads=[tk_kt, tk_fp], writes=[tk_kt])
            for cg in range(NCG):
                def consume(bank, btk, k1a, nk, o=o, cg=cg):
                    xs, xsk = xsring.next()
                    if (k1a // 8) % 2 == 0:
                        S.op("dve", lambda e: e.tensor_copy(xs[:, 0:nk * CG], bank[:, 0:nk * CG]), reads=[btk], writes=[xsk])
                    else:
                        S.op("act", lambda e: e.copy(xs[:, 0:nk * CG], bank[:, 0:nk * CG]), reads=[btk], writes=[xsk])
                    S.dma("sp", KH[o, cg, :, k1a * CG:(k1a + nk) * CG], xs[:, 0:nk * CG], reads=[xsk], writes=[tk_kh])

                fft_forward(st, kt[:, :, cg * CG:(cg + 1) * CG], tk_kt, 128, YPfs[cg % 2][0], YPfs[cg % 2][1], consume, alt=True)
        S.barrier()

    for _pass in (0,):
        with ExitStack() as st:
            wq = sb(st, "wq", [128, 8, 2816], BF16)
            wh = sb(st, "wh", [128, 8, 1536], BF16)
            g1 = sb(st, "g1", [128, D], F32)
            shwp = sb(st, "shwp", [128, 12, 3], F32)
            shbp = sb(st, "shbp", [128, 12], F32)
            tk_w = Tk()
            S.dma("sp", g1[:], I["g1r"][:, :], reads=[tk_in], writes=[tk_w])
            S.dma("sp", shwp[:], I["shortwp"][:, :, :], reads=[tk_in], writes=[tk_w])
            S.dma("sp", shbp[:], I["shortbp"][:, :], reads=[tk_in], writes=[tk_w])
            w_in_v = I["w_in"].rearrange("(k p) c -> p k c", p=128)
            for kc in range(8):
                S.dma("pool", wq[:, kc, :], w_in_v[:, kc, 1536:D_IN], reads=[tk_in], writes=[tk_w])
                S.dma("pool", wh[:, kc, :], w_in_v[:, kc, 0:1536], reads=[tk_in], writes=[tk_w])
            hT = [(sb(st, "hT%d" % i, [128, 8, 514], BF16), Tk()) for i in range(3)]
            for i in range(3):
                S.op("pool", lambda e: e.memset(hT[i][0][:], 0.0), writes=[hT[i][1]])
            xr = Ring(st, "xr", [128, D], F32, 8)
            xnr = Ring(st, "xnr", [128, D], BF16, 4)
            smr = Ring(st, "smr", [128, 4], F32, 8)
            junk = sb(st, "junk", [128, D], BF16)
            junk_tk = Tk()
            stg = Ring(st, "stg", [128, 4, 2304], BF16, 2)
            sgr = Ring(st, "sgr", [128, 512], BF16, 4)
            uer = Ring(st, "ue", [128, 514], F32, 3)
            ocr = Ring(st, "oc", [128, 512], F32, 3)
            obr = Ring(st, "ob", [128, 512], BF16, 3)

            for s in range(NSEQ):
                def xsrcs(g):
                    return [I["x"][s, g * 512 + m * 128:g * 512 + (m + 1) * 128, :] for m in range(4)]

                def normgroup(g, xts, phase="both", xns=None):
                    buf_, btk2_ = hT[g % 3]
                    return norm_group_to_T(xr, xnr, smr, xsrcs(g), buf_[:, :, 1:513], btk2_, [m * 128 for m in range(4)], junk,
                                           junk_tk, g1, tk_w, xts=xts, phase=phase, xns=xns)

                def halos(g):
                    b0, k0 = hT[g % 3]
                    if g == 0:
                        S.op("pool", lambda e: e.memset(b0[:, :, 0:1], 0.0), writes=[k0])
                    if g + 1 < 8:
                        b1, k1_ = hT[(g + 1) % 3]
                        S.op("pool", lambda e: e.tensor_copy(b0[:, :, 513:514], b1[:, :, 1:2]), reads=[k1_], writes=[k0])
                        S.op("pool", lambda e: e.tensor_copy(b1[:, :, 0:1], b0[:, :, 512:513]), reads=[k0], writes=[k1_])
                    else:
                        S.op("pool", lambda e: e.memset(b0[:, :, 513:514], 0.0), writes=[k0])

                xpre = {0: norm_load(xr, xsrcs(0), "pool"), 1: norm_load(xr, xsrcs(1), "pool")}
                normgroup(0, xpre.pop(0))
                normgroup(1, xpre.pop(1))
                halos(0)
                for g in range(8):
                    buf, btk_ = hT[g % 3]
                    if g + 2 < 8:
                        xpre[g + 2] = norm_load(xr, xsrcs(g + 2), "pool")
                    so, sok = stg.next()
                    def hy_u(ct):
                        bu, buk = psum((0, 4))
                        for kc in range(8):
                            mm(bu[:, :], wh[:, kc, ct * 128:(ct + 1) * 128], buf[:, kc, 1:513], kc == 0, kc == 7, [btk_, tk_w], [buk], kc == 7)
                        return bu, buk

                    def hy_h(ct):
                        bh, bhk = banks[4][0][:, (ct % 2) * 2:(ct % 2) * 2 + 2], banks[4][1]
                        for kc in range(8):
                            mm(bh, wh[:, kc, ct * 128:(ct + 1) * 128], buf[:, kc, 0:514:513], kc == 0, kc == 7, [btk_, tk_w], [bhk], kc == 7)
                        return bh, bhk

                    def hy_ew(ct, und, hnd):
                        bu, buk = und
                        bh, bhk = hnd
                        ue, uek = uer.next()
                        S.op("act", lambda e: e.copy(ue[:, 1:513], bu[:, :]), reads=[buk], writes=[uek])
                        S.op("act", lambda e: e.copy(ue[:, 0:514:513], bh), reads=[bhk], writes=[uek])
                        oc, ock = ocr.next()
                        S.op("act", lambda e: e.activation(out=oc[:], in_=ue[:, 1:513], func=AF.Identity, scale=shwp[:, ct, 1:2],
                                                           bias=shbp[:, ct:ct + 1]), reads=[uek, tk_w], writes=[ock])
                        S.op("dve", lambda e: e.scalar_tensor_tensor(out=oc[:], in0=ue[:, 0:512], scalar=shwp[:, ct, 0:1], in1=oc[:],
                                                                     op0=ALU.mult, op1=ALU.add), reads=[uek, tk_w, ock], writes=[ock])
                        ob_, obk_ = obr.next()
                        S.op("dve", lambda e: e.scalar_tensor_tensor(out=ob_[:], in0=ue[:, 2:514], scalar=shwp[:, ct, 2:3], in1=oc[:],
                                                                     op0=ALU.mult, op1=ALU.add), reads=[uek, tk_w, ock], writes=[obk_])
                        return ob_, obk_

                    def hy_tr(ct, ob_, obk_):
                        bt, btk2 = psum((5, 3))
                        bv = bt[:].bitcast(BF16)
                        for m in range(4):
                            S.op("pe", lambda e: e.transpose(bv[:, m * 128:(m + 1) * 128], ob_[:, m * 128:(m + 1) * 128], ident[:]),
                                 reads=[obk_, tk_c], writes=[btk2], sig=(m == 3))
                        S.op("dve", lambda e: e.tensor_copy(so[:, :, ct * 128:(ct + 1) * 128], bv[:, 0:512].rearrange("p (m c) -> p m c", m=4)),
                             reads=[btk2], writes=[sok])

                    pend = [hy_u(0), hy_u(1)]
                    hcur = hy_h(0)
                    nst = None
                    for ct in range(12):
                        if ct + 2 < 12:
                            pend.append(hy_u(ct + 2))
                        ob_, obk_ = hy_ew(ct, pend.pop(0), hcur)
                        if ct + 1 < 12:
                            hcur = hy_h(ct + 1)
                        hy_tr(ct, ob_, obk_)
                    nst = None
                    if g + 2 < 8:
                        nst = normgroup(g + 2, xpre.pop(g + 2), phase="stats")
                    for m in range(4):
                        c1 = 1 + m * 128
                        for (c0, cw) in ((0, 512), (512, 256)):
                            bank, bk = psum()
                            for kc in range(8):
                                mm(bank[:, 0:cw], buf[:, kc, c1:c1 + 128], wq[:, kc, c0:c0 + cw], kc == 0, kc == 7,
                                   [btk_, tk_w], [bk], kc == 7)
                            S.op("act", lambda e: e.copy(so[:, m, 1536 + c0:1536 + c0 + cw], bank[:, 0:cw]), reads=[bk],
                                 writes=[sok])
                    S.dma("sp", PROJ[s, g * 512:(g + 1) * 512, :].rearrange("(m p) c -> p m c", p=128), so[:], reads=[sok],
                          writes=[tk_proj])
                    if g + 2 < 8:
                        normgroup(g + 2, None, phase="trans", xns=nst[1])
                    if g + 1 < 8:
                        halos(g + 1)
                    for gc in range(16):
                        bank, bk = psum()
                        for kc in range(8):
                            mm(bank[:, :], wq[:, kc, 768 + gc * 128:768 + (gc + 1) * 128], buf[:, kc, 1:513], kc == 0, kc == 7,
                               [btk_, tk_w], [bk], kc == 7)
                        sg_, sgk = sgr.next()
                        S.op("act", lambda e: e.activation(out=sg_[:], in_=bank[:, :], func=AF.Sigmoid), reads=[bk],
                             writes=[sgk])
                        S.dma("sp", SG[s, gc * 128:(gc + 1) * 128, g * 512:(g + 1) * 512], sg_[:], reads=[sgk], writes=[tk_sg])
            S.barrier()

        with ExitStack() as st:
            QT = sb(st, "QT", [128, 32, 4, 128], BF16)
            KT = sb(st, "KT", [128, L], BF16)
            VX = sb(st, "VX", [128, 32, 2, 65], BF16)
            rope = sb(st, "rope", [128, 32, 2, 8], F32)
            negm = sb(st, "negm", [128, 2, 4, 128], BF16)
            esk = sb(st, "esk", [128, 8], F32)
            tk_ac = Tk()
            tk_qt, tk_kt2, tk_vx = Tk(), Tk(), Tk()
            S.dma("sp", rope[:], I["rope"][:, :, :, :], reads=[tk_in], writes=[tk_ac])
            S.dma("sp", negm[:], I["negm"][:, :, :, :], reads=[tk_in], writes=[tk_ac])
            S.dma("sp", esk[:], I["sinkr"][:, :], reads=[tk_in], writes=[tk_ac])
            S.op("act", lambda e: e.activation(out=esk[:], in_=esk[:], func=AF.Exp), reads=[tk_ac], writes=[tk_ac])
            S.op("pool", lambda e: e.memset(VX[:], 1.0), writes=[tk_vx])
            qr = Ring(st, "qkv", [128, 768], BF16, 3)
            qfr = Ring(st, "qkf", [128, 10, 16], F32, 2)
            qpr = Ring(st, "qp", [128, 512], BF16, 2)
            rtr = Ring(st, "rt", [128, 4, 10, 8], F32, 2)
            for s in range(NSEQ):
                for m in range(32):
                    qt_, qk = qr.next()
                    S.dma("sp", qt_[:], PROJ[s, m * 128:(m + 1) * 128, 1536:2304], reads=[tk_proj], writes=[qk])
                    qv = qt_[:, 0:640].rearrange("p (h d) -> p h d", d=64)
                    qf, qfk = qfr.next()
                    S.op("dve", lambda e: e.tensor_copy(qf[:], qv[:, :, 0:16]), reads=[qk], writes=[qfk])
                    rt, rtk = rtr.next()
                    cosb = rope[:, m, 0:1, :].broadcast_to([128, 10, 8])
                    sinb = rope[:, m, 1:2, :].broadcast_to([128, 10, 8])
                    a_ = qf[:, :, 0:8]
                    b_ = qf[:, :, 8:16]
                    S.op("dve", lambda e: e.tensor_tensor(rt[:, 0, :, :], a_, cosb, op=ALU.mult), reads=[qfk, tk_ac], writes=[rtk])
                    S.op("dve", lambda e: e.tensor_tensor(rt[:, 1, :, :], b_, sinb, op=ALU.mult), reads=[qfk, tk_ac], writes=[rtk])
                    S.op("dve", lambda e: e.tensor_tensor(rt[:, 2, :, :], b_, cosb, op=ALU.mult), reads=[qfk, tk_ac], writes=[rtk])
                    S.op("dve", lambda e: e.tensor_tensor(rt[:, 3, :, :], a_, sinb, op=ALU.mult), reads=[qfk, tk_ac], writes=[rtk])
                    S.op("dve", lambda e: e.tensor_tensor(qv[:, :, 0:8], rt[:, 0, :, :], rt[:, 1, :, :], op=ALU.subtract),
                         reads=[rtk], writes=[qk])
                    S.op("dve", lambda e: e.tensor_tensor(qv[:, :, 8:16], rt[:, 2, :, :], rt[:, 3, :, :], op=ALU.add),
                         reads=[rtk], writes=[qk])
                    bank, bk = psum()
                    bv = bank[:].bitcast(BF16)
                    qp, qpk = qpr.next()
                    S.op("act", lambda e: e.copy(qp[:].rearrange("p (j g d) -> p j g d", j=4, g=2),
                                                 qt_[:, 0:512].rearrange("p (g j d) -> p j g d", g=2, j=4)),
                         reads=[qk], writes=[qpk])
                    for j in range(4):
                        S.op("pe", lambda e: e.transpose(bv[:, j * 128:(j + 1) * 128], qp[:, j * 128:(j + 1) * 128], ident[:]),
                             reads=[qpk, tk_c], writes=[bk], sig=False)
                    S.op("pe", lambda e: e.transpose(bv[:, 512:640], qt_[:, 512:640], ident[:]), reads=[qk, tk_c], writes=[bk])
                    S.op("act", lambda e: e.copy(QT[:, m, :, :].rearrange("p j t -> p (j t)"), bv[:, 0:512]),
                         reads=[bk], writes=[tk_qt])
                    S.op("act", lambda e: e.copy(KT[:, m * 128:(m + 1) * 128], bv[:, 512:640]), reads=[bk], writes=[tk_kt2])
                    S.op("pool", lambda e: e.tensor_copy(VX[:, m, :, 0:64], qt_[:, 640:768].rearrange("p (g d) -> p g d", g=2)),
                         reads=[qk], writes=[tk_vx])
                ptr = Ring(st, "pt", [128, 512], BF16, 8)
                yar = Ring(st, "ya", [128, 512], BF16, 2)
                dnr = Ring(st, "dn", [128, 8], F32, 2)
                yatg = Ring(st, "yatg", [128, 4, 512], BF16, 2)
                SP, OP = (0, 6), (6, 2)

                def s_step(i, gk):
                    pr = slice(gk * 64, gk * 64 + 64)
                    chunks = [c for c in (-1, 0, 1) if 0 <= i + c < 32]
                    pts = []
                    for c in chunks:
                        sbank, sbk = psum(SP)
                        kt0 = (i + c) * 128
                        mm(sbank[:, :], KT[pr, kt0:kt0 + 128], QT[pr, i, :, :].rearrange("p j t -> p (j t)"), True, c == 0,
                           [tk_kt2, tk_qt], [sbk], c == 0)
                        if c != 0:
                            mm(sbank[:, :], ident[:], negm[:, 0 if c < 0 else 1, :, :].rearrange("p j t -> p (j t)"), False, True,
                               [tk_c, tk_ac], [sbk], True)
                        pt, ptk = ptr.next()
                        S.op("act", lambda e: e.activation(out=pt[:], in_=sbank[:, :], func=AF.Exp, scale=0.125),
                             reads=[sbk], writes=[ptk])
                        pts.append((c, pt, ptk))
                    return pts

                def pv_step(i, gk, pts, ya, yak, dn, dnk):
                    ob, obk = psum(OP)
                    for j in range(4):
                        for ci, (c, pt, ptk) in enumerate(pts):
                            mm(ob[:, j * 128:j * 128 + 65], pt[:, j * 128:(j + 1) * 128], VX[:, i + c, gk, :],
                               ci == 0, ci == len(pts) - 1, [ptk, tk_vx], [obk], (j == 3) and (ci == len(pts) - 1))
                    ov = ob[:, :].rearrange("p (h d) -> p h d", d=128)
                    S.op("dve", lambda e: e.tensor_tensor(dn[:, gk * 4:gk * 4 + 4], ov[:, :, 64], esk[:, gk * 4:gk * 4 + 4], op=ALU.add),
                         reads=[obk, tk_ac], writes=[dnk])
                    S.op("dve", lambda e: e.reciprocal(dn[:, gk * 4:gk * 4 + 4], dn[:, gk * 4:gk * 4 + 4]), reads=[dnk], writes=[dnk])
                    S.op("dve", lambda e: e.tensor_tensor(
                        ya[:, gk * 256:(gk + 1) * 256].rearrange("p (h d) -> p h d", d=64), ov[:, :, 0:64],
                        dn[:, gk * 4:gk * 4 + 4].unsqueeze(2).broadcast_to([128, 4, 64]), op=ALU.mult),
                         reads=[obk, dnk], writes=[yak])

                work = [(i, gk) for i in range(32) for gk in range(2)]
                pend = s_step(*work[0])
                cur_ya = None
                yg = ygk = None
                for w, (i, gk) in enumerate(work):
                    nxt = s_step(*work[w + 1]) if w + 1 < len(work) else None
                    if gk == 0:
                        cur_ya = yar.next() + dnr.next()
                        if i % 4 == 0:
                            yg, ygk = yatg.next()
                    ya, yak, dn, dnk = cur_ya
                    pv_step(i, gk, pend, ya, yak, dn, dnk)
                    pend = nxt
                    if gk == 1:
                        bank, bk = psum(OP)
                        bv = bank[:].bitcast(BF16)
                        for kc in range(4):
                            S.op("pe", lambda e: e.transpose(bv[:, kc * 128:(kc + 1) * 128], ya[:, kc * 128:(kc + 1) * 128], ident[:]),
                                 reads=[yak, tk_c], writes=[bk], sig=(kc == 3))
                        mq = i % 4
                        S.op("act", lambda e: e.copy(yg[:, :, mq * 128:(mq + 1) * 128], bv[:, 0:512].rearrange("p (k t) -> p k t", k=4)),
                             reads=[bk], writes=[ygk])
                        if mq == 3:
                            g = i // 4
                            S.dma("sp", YAT[s].rearrange("(k p) t -> p k t", p=128)[:, :, g * 512:(g + 1) * 512], yg[:], reads=[ygk],
                                  writes=[tk_yat])
            S.barrier()

        with ExitStack() as st:
            load_fft_consts(st)
            fv, fe = FC["fv"], FC["fe"]
            tvs = [(sb(st, "tv%d" % i, [64, 64, CG], BF16), Tk()) for i in range(2)]
            gater = Ring(st, "gate", [64, 64, CG], BF16, 2)
            YPs = [(sb(st, "YP%d" % i, [128, 130 * CG], BF16), Tk()) for i in range(2)]
            ABk = [(Tk(), Tk()) for i in range(2)]
            XS = sb(st, "XS", [128, 72, CG], BF16)
            tk_xs = Tk()
            DDT = sb(st, "DDT", [NK1, 2 * 64 * CG], BF16)
            DD = DDT[:, :].rearrange("p (r n c) -> p r n c", r=2, n=64)
            tk_dd = Tk()
            khr = Ring(st, "khb", [128, 2, NK1, CG], BF16, 2)
            yT = sb(st, "yT", [64, L], BF16)
            tk_yT = Tk()

            def half1(step):
                s_, cg, o, slot = step["s"], step["cg"], step["o"], step["slot"]
                tv, tkv = tvs[slot]
                PRv = PROJ[s_].rearrange("(a b) c -> a b c", b=64)
                if o == 0:
                    S.dma("sp", tv[:], PRv[:, :, cg * CG:(cg + 1) * CG], reads=[tk_proj], writes=[tkv])
                gate, gate_tk = gater.next()
                gc0 = (512 if o == 0 else 1024) + cg * CG
                S.dma("sp", gate[:], PRv[:, :, gc0:gc0 + CG], reads=[tk_proj], writes=[gate_tk])
                kh, khk = khr.next()
                S.dma("sp", kh[:, 0, :, :].rearrange("p a c -> p (a c)"), KH[o, cg, :, :], reads=[tk_kh], writes=[khk])
                S.dma("sp", kh[0:64, 1, :, :].rearrange("p a c -> p (a c)"), KH[o, cg, 64:128, :], reads=[tk_kh], writes=[khk])
                S.dma("sp", kh[64:128, 1, :, :].rearrange("p a c -> p (a c)"), KH[o, cg, 0:64, :], reads=[tk_kh], writes=[khk])
                step.update(gate=gate, gate_tk=gate_tk, kh=kh, khk=khk)
                YP, tk_yp = YPs[step["idx"] % 2]
                tk_a, tk_b = ABk[step["idx"] % 2]
                step.update(YP=YP, tk_yp=tk_yp, tk_a=tk_a, tk_b=tk_b)

                fft_forward(st, tv, tkv, 64, YP, tk_yp, None, ywaits=[tk_a, tk_b], part="s1")

            def half1b(step):
                tv, tkv = tvs[step["slot"]]

                def consume(bank, btk, k1a, nk):
                    S.op("act", lambda e: e.copy(XS[:, k1a:k1a + nk, :].rearrange("p a c -> p (a c)"),
                                                 bank[:, 0:nk * CG]), reads=[btk], writes=[tk_xs])

                fft_forward(st, tv, tkv, 64, step["YP"], step["tk_yp"], consume, part="s2")

            def half2a(step):
                s_, cg, o, slot = step["s"], step["cg"], step["o"], step["slot"]
                tv, tkv = tvs[slot]
                kh, khk, gate, gate_tk = step["kh"], step["khk"], step["gate"], step["gate_tk"]
                YP, tk_yp = step["YP"], step["tk_yp"]
                Pv = YP[:, 0:2 * CG * NK1].rearrange("p (r c k) -> p r c k", r=2, c=CG)
                Xs = XS[:, 0:NK1, :]
                tk_a, tk_b = step["tk_a"], step["tk_b"]
                S.op("dve", lambda e: e.tensor_tensor(Pv[:, 0, :, :].rearrange("p c k -> p k c"), Xs, kh[:, 0, :, :], op=ALU.mult),
                     reads=[tk_xs, khk], writes=[tk_a], waits=[tk_yp])
                S.op("pool", lambda e: e.tensor_tensor(Pv[:, 1, :, :].rearrange("p c k -> p k c"), Xs, kh[:, 1, :, :], op=ALU.mult),
                     reads=[tk_xs, khk], writes=[tk_b], waits=[tk_yp])

            def half2b(step):
                s_, cg, o, slot = step["s"], step["cg"], step["o"], step["slot"]
                tv, tkv = tvs[slot]
                gate, gate_tk = step["gate"], step["gate_tk"]
                YP, tk_yp = step["YP"], step["tk_yp"]
                Pv = YP[:, 0:2 * CG * NK1].rearrange("p (r c k) -> p r c k", r=2, c=CG)
                for c0 in range(0, CG, 4):
                    bank, bk = psum()
                    for cc in range(4):
                        c = c0 + cc
                        mm(bank[0:NK1, cc * 128:(cc + 1) * 128], Pv[:, 0, c, :], fv[:, 0, :], True, False, [step["tk_a"], tk_f], [bk], False)
                        mm(bank[0:NK1, cc * 128:(cc + 1) * 128], Pv[:, 1, c, :], fv[:, 1, :], False, True, [step["tk_b"], tk_f], [bk], cc == 3)
                    S.op("act", lambda e: e.copy(DD[:, :, :, c0:c0 + 4].rearrange("p r n c -> p (r n) c"),
                                                 bank[0:NK1, :].rearrange("p (c q) -> p q c", c=4)), reads=[bk], writes=[tk_dd])

            def half2c(step):
                s_, cg, o, slot = step["s"], step["cg"], step["o"], step["slot"]
                tv, tkv = tvs[slot]
                gate, gate_tk = step["gate"], step["gate_tk"]
                for n0 in range(0, 64, 8):
                    bank, bk = psum()
                    for nn in range(8):
                        n2 = n0 + nn
                        mm(bank[0:64, nn * CG:(nn + 1) * CG], fe[:, n2, 0, :], DD[:, 0, n2, :], True, False, [tk_dd, tk_f], [bk], False)
                        mm(bank[0:64, nn * CG:(nn + 1) * CG], fe[:, n2, 1, :], DD[:, 1, n2, :], False, True, [tk_dd, tk_f], [bk], nn == 7)
                    S.op("dve", lambda e: e.tensor_tensor(tv[:, n0:n0 + 8, :].rearrange("p a c -> p (a c)"), bank[0:64, :],
                                                          gate[:, n0:n0 + 8, :].rearrange("p a c -> p (a c)"), op=ALU.mult),
                         reads=[bk, gate_tk], writes=[tkv])
                if o == 1:
                    yTv = yT[:].rearrange("p (a b) -> p a b", b=64)
                    for n0 in range(0, 64, 16):
                        bank, bk = psum()
                        bv = bank[:].bitcast(BF16)
                        for nn in range(16):
                            S.op("pe", lambda e: e.transpose(bv[0:CG, nn * 64:(nn + 1) * 64], tv[:, n0 + nn, :], ident[0:64, 0:64]),
                                 reads=[tkv, tk_c], writes=[bk], sig=(nn == 15))
                        S.op("act", lambda e: e.copy(yTv[:, :, n0:n0 + 16], bv[0:CG, :].rearrange("p (b a) -> p a b", b=16)),
                             reads=[bk], writes=[tk_yT])
                    S.dma("sp", YHT[s_, cg * CG:(cg + 1) * CG, :], yT[:], reads=[tk_yT], writes=[tk_yht])

            steps = []
            for s in range(NSEQ):
                for pair in range(NCG // 2):
                    for o in range(2):
                        for slot in range(2):
                            steps.append(dict(s=s, cg=2 * pair + slot, o=o, slot=slot, idx=len(steps)))
            half1(steps[0])
            half1b(steps[0])
            for k in range(len(steps)):
                half2a(steps[k])
                if k + 1 < len(steps):
                    half1(steps[k + 1])
                half2b(steps[k])
                if k + 1 < len(steps):
                    half1b(steps[k + 1])
                half2c(steps[k])
            S.barrier()

        with ExitStack() as st:
            wuh = sb(st, "wuh", [128, 4, D], BF16)
            wua = sb(st, "wua", [128, 4, D], BF16)
            wo_ = sb(st, "wo_", [128, 8, D], BF16)
            tk_w4 = Tk()
            S.dma("pool", wuh[:], I["w_up_h"].rearrange("(k p) c -> p k c", p=128), reads=[tk_in], writes=[tk_w4])
            S.dma("pool", wua[:], I["w_up_a"].rearrange("(k p) c -> p k c", p=128), reads=[tk_in], writes=[tk_w4])
            S.dma("pool", wo_[:], I["w_o"].rearrange("(k p) c -> p k c", p=128), reads=[tk_in], writes=[tk_w4])
            yhr = Ring(st, "yhg", [128, 4, 512], BF16, 2)
            yagr = Ring(st, "yag", [128, 4, 512], BF16, 2)
            sggr = Ring(st, "sgg", [128, 16, 512], BF16, 2)
            mTr = Ring(st, "mT", [128, 8, 512], BF16, 2)
            t1r = Ring(st, "m1", [128, 512], F32, 2)
            t2r = Ring(st, "m2", [128, 512], F32, 2)
            xr = Ring(st, "x4", [128, D], F32, 8)
            x1r = Ring(st, "x14", [128, D], F32, 3)
            for s in range(NSEQ):
                def prefetch4(g):
                    ts = slice(g * 512, (g + 1) * 512)
                    yh, yhk = yhr.next()
                    yag, yagk = yagr.next()
                    sgg, sggk = sggr.next()
                    S.dma("sp", yh[:], YHT[s].rearrange("(k p) t -> p k t", p=128)[:, :, ts], reads=[tk_yht], writes=[yhk])
                    S.dma("sp", yag[:], YAT[s].rearrange("(k p) t -> p k t", p=128)[:, :, ts], reads=[tk_yat], writes=[yagk])
                    S.dma("sp", sgg[:], SG[s].rearrange("(k p) t -> p k t", p=128)[:, :, ts], reads=[tk_sg], writes=[sggk])
                    xts = []
                    for m in range(4):
                        xt, xtk = xr.next()
                        t0 = g * 512 + m * 128
                        S.dma("sp", xt[:], I["x"][s, t0:t0 + 128, :], reads=[tk_in], writes=[xtk])
                        xts.append((xt, xtk))
                    return (yh, yhk, yag, yagk, sgg, sggk, xts)

                nxt4 = prefetch4(0)
                for g in range(8):
                    yh, yhk, yag, yagk, sgg, sggk, xts = nxt4
                    if g + 1 < 8:
                        nxt4 = prefetch4(g + 1)
                    mT, mTk = mTr.next()
                    for mc in range(8):
                        bh, bhk = psum()
                        ba, bak = psum()
                        for kc in range(4):
                            mm(bh[:, :], wuh[:, kc, mc * 128:(mc + 1) * 128], yh[:, kc, :], kc == 0, kc == 3, [tk_w4, yhk], [bhk], kc == 3)
                        for kc in range(4):
                            mm(ba[:, :], wua[:, kc, mc * 128:(mc + 1) * 128], yag[:, kc, :], kc == 0, kc == 3, [tk_w4, yagk], [bak], kc == 3)
                        t1, t1k = t1r.next()
                        t2, t2k = t2r.next()
                        S.op("dve", lambda e: e.tensor_tensor(t1[:], bh[:, :], sgg[:, mc, :], op=ALU.mult), reads=[bhk, sggk], writes=[t1k])
                        S.op("dve", lambda e: e.tensor_tensor(t2[:], ba[:, :], sgg[:, 8 + mc, :], op=ALU.mult), reads=[bak, sggk], writes=[t2k])
                        S.op("pool", lambda e: e.tensor_tensor(mT[:, mc, :], t1[:], t2[:], op=ALU.add), reads=[t1k, t2k], writes=[mTk])
                    for m in range(4):
                        t0 = g * 512 + m * 128
                        xt, xtk = xts[m]
                        x1, x1k = x1r.next()
                        for half in range(2):
                            bank, bk = psum()
                            for kc in range(8):
                                mm(bank[:, :], mT[:, kc, m * 128:(m + 1) * 128], wo_[:, kc, half * 512:(half + 1) * 512], kc == 0, kc == 7,
                                   [mTk, tk_w4], [bk], kc == 7)
                            S.op("dve", lambda e: e.tensor_tensor(x1[:, half * 512:(half + 1) * 512], bank[:, :],
                                                                  xt[:, half * 512:(half + 1) * 512], op=ALU.add),
                                 reads=[bk, xtk], writes=[x1k])
                        S.dma("sp", X1[s * L + t0:s * L + t0 + 128, :], x1[:], reads=[x1k], writes=[tk_x1])
            S.barrier()

    with ExitStack() as st:
        wd = sb(st, "wd", [128, NFF, D], BF16)
        g2 = sb(st, "g2", [128, D], F32)
        gf = sb(st, "gf", [128, D], F32)
        tk_w5 = Tk()
        S.dma("pool", wd[:], I["w_down"].rearrange("(k p) c -> p k c", p=128), reads=[tk_in], writes=[tk_w5])
        S.dma("sp", g2[:], I["g2r"][:, :], reads=[tk_in], writes=[tk_w5])
        S.dma("sp", gf[:], I["gfr"][:, :], reads=[tk_in], writes=[tk_w5])
        wgr = Ring(st, "wg", [128, 8, 128], BF16, 4)
        wur = Ring(st, "wu", [128, 8, 128], BF16, 4)
        x1r = Ring(st, "x15", [128, D], F32, 8)
        xnr = Ring(st, "xn5", [128, D], BF16, 4)
        smr = Ring(st, "sm5", [128, 4], F32, 8)
        junk = sb(st, "junk5", [128, D], BF16)
        junk_tk = Tk()
        h2r = Ring(st, "h2T", [128, 8, 512], BF16, 2)
        actr = Ring(st, "actT", [128, NFF, 512], BF16, 1)
        sgr = Ring(st, "sil", [128, 512], BF16, 3)
        xfr = Ring(st, "xf5", [128, D], F32, 2)
        outr = Ring(st, "out5", [128, D], F32, 2)
        wg_v = I["w_gate"].rearrange("(k p) c -> p k c", p=128)
        wu_v = I["w_up"].rearrange("(k p) c -> p k c", p=128)
        def norm5_stats(g):
            h2, h2k = h2r.next()
            srcs = [X1[g * 512 + m * 128:g * 512 + (m + 1) * 128, :] for m in range(4)]
            xtiles, xns = norm_group_to_T(x1r, xnr, smr, srcs, h2, h2k, [m * 128 for m in range(4)], junk, junk_tk, g2, tk_w5,
                                          phase="stats")
            return h2, h2k, xtiles, xns

        def norm5_trans(st5):
            h2, h2k, xtiles, xns = st5
            norm_trans(xns, h2, h2k, [m * 128 for m in range(4)])

        nxt5 = norm5_stats(0)
        norm5_trans(nxt5)
        for g in range(NSEQ * 8):
            h2, h2k, xtiles, _ = nxt5
            act_, actk = actr.next()
            for j in range(NFF):
                wg, wgk = wgr.next()
                wu, wuk = wur.next()
                S.dma("pool", wg[:], wg_v[:, :, j * 128:(j + 1) * 128], reads=[tk_in], writes=[wgk])
                S.dma("pool", wu[:], wu_v[:, :, j * 128:(j + 1) * 128], reads=[tk_in], writes=[wuk])
                bg, bgk = psum()
                bu, buk = psum()
                for kc in range(8):
                    mm(bg[:, :], wg[:, kc, :], h2[:, kc, :], kc == 0, kc == 7, [wgk, h2k], [bgk], kc == 7)
                for kc in range(8):
                    mm(bu[:, :], wu[:, kc, :], h2[:, kc, :], kc == 0, kc == 7, [wuk, h2k], [buk], kc == 7)
                sl, slk = sgr.next()
                S.op("act", lambda e: e.activation(out=sl[:], in_=bg[:, :], func=AF.Silu), reads=[bgk], writes=[slk])
                S.op("dve", lambda e: e.tensor_tensor(act_[:, j, :], bu[:, :], sl[:], op=ALU.mult), reads=[buk, slk], writes=[actk])
            if g + 1 < NSEQ * 8:
                nxt5 = norm5_stats(g + 1)
            for m in range(4):
                if m == 2 and g + 1 < NSEQ * 8:
                    norm5_trans(nxt5)
                t0 = g * 512 + m * 128
                xt, xtk = xtiles[m]
                xf, xfk = xfr.next()
                for half in range(2):
                    bank, bk = psum()
                    for j in range(NFF):
                        mm(bank[:, :], act_[:, j, m * 128:(m + 1) * 128], wd[:, j, half * 512:(half + 1) * 512], j == 0, j == NFF - 1,
                           [actk, tk_w5], [bk], j == NFF - 1)
                    S.op("dve", lambda e: e.tensor_tensor(xf[:, half * 512:(half + 1) * 512], bank[:, :],
                                                          xt[:, half * 512:(half + 1) * 512], op=ALU.add),
                         reads=[bk, xtk], writes=[xfk])
                sm, smk = smr.next()
                S.op("act", lambda e: e.activation(out=junk[:], in_=xf[:], func=AF.Square, accum_out=sm[:, 0:1]),
                     reads=[xfk], writes=[junk_tk, smk])
                S.op("dve", lambda e: e.tensor_scalar(sm[:, 1:2], sm[:, 0:1], 1.0 / D, EPS, op0=ALU.mult, op1=ALU.add),
                     reads=[smk], writes=[smk])
                S.op("act", lambda e: e.activation(out=sm[:, 2:3], in_=sm[:, 1:2], func=AF.Sqrt), reads=[smk], writes=[smk])
                S.op("dve", lambda e: e.reciprocal(sm[:, 3:4], sm[:, 2:3]), reads=[smk], writes=[smk])
                ot, otk = outr.next()
                S.op("dve", lambda e: e.scalar_tensor_tensor(out=ot[:], in0=xf[:], scalar=sm[:, 3:4], in1=gf[:], op0=ALU.mult,
                                                             op1=ALU.mult), reads=[xfk, smk, tk_w5], writes=[otk])
                S.dma("sp", yout[t0 // L, t0 % L:t0 % L + 128, :], ot[:], reads=[otk], writes=[tk_out])
    S.barrier(engines=("sp",))
    es.close()
    return nc, S


_CACHE = {}


def _prep_inputs(inp):
    f32 = np.float32
    g = lambda k: np.ascontiguousarray(np.asarray(inp[k], dtype=f32))
    shared = {
        "w_in": g("w_in")[0], "g1r": np.ascontiguousarray(np.broadcast_to(g("norm1_g")[0][None], (128, D))),
        "shortwp": np.ascontiguousarray(g("short_w")[0].reshape(3, 12, 128).transpose(2, 1, 0)),
        "shortbp": np.ascontiguousarray(g("short_b")[0].reshape(12, 128).T),
        "fw0": g("filt_w0")[0], "fb0": np.ascontiguousarray(g("filt_b0")[0].reshape(64, 1)),
        "fwi": g("filt_w_inner")[0], "fbi": np.ascontiguousarray(g("filt_b_inner")[0].T),
        "ffreq": np.ascontiguousarray(g("filt_freq")[0].reshape(64, 1)), "fwo": g("filt_w_out")[0],
        "hbias": np.ascontiguousarray(g("hyena_bias")[0].reshape(1, 1024)),
        "sinkr": np.ascontiguousarray(np.broadcast_to(g("sink_logit")[0][None], (128, 8))),
        "w_up_h": g("w_up_hyena")[0], "w_up_a": g("w_up_attn")[0], "w_o": g("w_out")[0],
        "g2r": np.ascontiguousarray(np.broadcast_to(g("norm2_g")[0][None], (128, D))),
        "w_gate": g("w_ff_gate")[0], "w_up": g("w_ff_up")[0], "w_down": g("w_ff_down")[0],
        "gfr": np.ascontiguousarray(np.broadcast_to(g("final_g")[None], (128, D))),
    }
    shared.update(_consts())
    return shared


def kernel(**inputs):
    xp = np.asarray(inputs["x_prompt"], dtype=np.float32)
    xs = np.asarray(inputs["x_sample"], dtype=np.float32)
    shared = _prep_inputs(inputs)
    if "nc" not in _CACHE:
        _CACHE["nc"] = build()[0]
    nc = _CACHE["nc"]
    in_maps = []
    for c in range(8):
        if c < 4:
            xc = xp[2 * c:2 * c + 2]
        else:
            xc = np.stack([xs[c - 4], xs[c - 4]], 0)
        m = dict(shared)
        m["x"] = np.ascontiguousarray(xc)
        in_maps.append(m)
    res = run_bass_kernel_spmd(nc, in_maps, core_ids=list(range(8)))
    yp = np.concatenate([res.results[c]["y"] for c in range(4)], 0).astype(np.float32)
    ys = np.stack([res.results[c]["y"][0] for c in range(4, 8)], 0).astype(np.float32)
    return (yp, ys)
```

```python
import math
from contextlib import ExitStack
import numpy as np
import ml_dtypes
import concourse.bass as bass
import concourse.mybir as mybir
from concourse.bass_utils import run_bass_kernel_spmd

F32 = mybir.dt.float32
BF16 = mybir.dt.bfloat16
AF = mybir.ActivationFunctionType
ALU = mybir.AluOpType
NPBF = ml_dtypes.bfloat16

D = 1024
L = 4096
NSEQ = 2
DH = 512
D_IN = 4352
D_FF = 2816
NFF = 22
NK1 = 65
CG = 64
NCG = 8
EPS = 1e-6
MAGIC = 12582912.0
TWO_PI = 2.0 * math.pi


class Tk:
    __slots__ = ("w", "r")

    def __init__(self):
        self.w = None
        self.r = {}


class Sched:
    def __init__(self, nc, es, ndma=64):
        self.nc = nc
        self.eng = {"pe": nc.tensor, "dve": nc.vector, "act": nc.scalar, "pool": nc.gpsimd, "sp": nc.sync}
        self.sem = {k: es.enter_context(nc.semaphore("s_" + k)) for k in self.eng}
        self.cnt = {k: 0 for k in self.eng}
        self.waited = {k: {} for k in self.eng}
        self.ndma = ndma
        for i in range(ndma):
            self.sem[("d", i)] = es.enter_context(nc.semaphore("s_d%d" % i))
            self.cnt[("d", i)] = 0
        self.dpool = {"sp": list(range(0, ndma - 16)), "pool": list(range(ndma - 16, ndma))}
        self.dnext = {"sp": 0, "pool": 0}
        self.ninstr = 0

    def _deps(self, reads, writes):
        deps = {}

        def add(k, v):
            if deps.get(k, 0) < v:
                deps[k] = v

        for t in reads:
            if t.w is not None:
                add(*t.w)
        for t in writes:
            if t.w is not None:
                add(*t.w)
            for k, v in t.r.items():
                add(k, v)
        return deps

    def _wait(self, e, deps):
        w = self.waited[e]
        for k, v in deps.items():
            if k == e and e == "pe":
                continue
            if w.get(k, 0) >= v:
                continue
            if isinstance(k, tuple):
                v = self.cnt[k]
            self.eng[e].wait_ge(self.sem[k], v)
            w[k] = v
            self.ninstr += 1

    def _mark(self, ev, reads, writes):
        for t in writes:
            t.w = ev
            t.r = {}
        for t in reads:
            k, v = ev
            if t.r.get(k, 0) < v:
                t.r[k] = v

    def op(self, e, fn, reads=(), writes=(), sig=True, waits=()):
        self._wait(e, self._deps(reads, list(writes) + list(waits)))
        ins = fn(self.eng[e])
        self.ninstr += 1
        if sig:
            self.cnt[e] += 1
            ins.then_inc(self.sem[e], 1)
            ev = (e, self.cnt[e])
        else:
            ev = (e, self.cnt[e] + 1)
        self._mark(ev, reads, writes)

    def dma(self, q, out, in_, reads=(), writes=()):
        self._wait(q, self._deps(reads, writes))
        pl = self.dpool[q]
        i = pl[self.dnext[q] % len(pl)]
        self.dnext[q] += 1
        k = ("d", i)
        self.cnt[k] += 16
        self.eng[q].dma_start(out=out, in_=in_).then_inc(self.sem[k], 16)
        self.ninstr += 1
        self._mark((k, self.cnt[k]), reads, writes)

    def barrier(self, engines=("pe", "dve", "act", "pool", "sp")):
        deps = {k: v for k, v in self.cnt.items() if v > 0}
        for e in engines:
            self._wait(e, dict(deps))


def _consts():
    c = {}
    c["ident"] = np.eye(128, dtype=np.float32).astype(NPBF)
    inv = (1.0 / (500000.0 ** (np.arange(0, 16, 2, dtype=np.float32) / 16.0))).astype(np.float32)
    pos = np.arange(L, dtype=np.float32)
    ang = (pos[:, None] * inv[None, :]).astype(np.float32)
    cs = np.stack([np.cos(ang), np.sin(ang)], 1).astype(np.float32)
    c["rope"] = np.ascontiguousarray(cs.reshape(32, 128, 2, 8).transpose(1, 0, 2, 3))
    j = np.arange(128)[:, None]
    i = np.arange(128)[None, :]
    neg = np.stack([np.where(j >= i, 0.0, -30000.0), np.where(j <= i, 0.0, -30000.0)], 1)
    c["negm"] = np.ascontiguousarray(np.broadcast_to(neg[:, :, None, :], (128, 2, 4, 128))).astype(np.float32).astype(NPBF)
    n1 = np.arange(128, dtype=np.float64)
    k1 = np.arange(NK1, dtype=np.float64)
    th = TWO_PI * n1[:, None] * k1[None, :] / 128.0
    c["fw1"] = np.stack([np.cos(th), -np.sin(th)], 1).astype(np.float32).astype(NPBF)
    n2 = np.arange(64, dtype=np.float64)
    k2 = np.arange(64, dtype=np.float64)
    th = TWO_PI * (n2[:, None, None] * k2[None, None, :] / 64.0 + n2[:, None, None] * k1[None, :, None] / 8192.0)
    mr, mi = np.cos(th), -np.sin(th)
    c["fm2"] = np.stack([np.concatenate([mr, mi], -1), np.concatenate([-mi, mr], -1)], 2).astype(np.float32).astype(NPBF)
    ph = TWO_PI * k2[:, None] * n2[None, :] / 64.0
    vr, vi = np.cos(ph), np.sin(ph)
    fv0 = np.concatenate([vr, vi], 1)
    fv1 = np.concatenate([-vi, vr], 1)
    c["fv"] = np.stack([np.concatenate([fv0, -fv0], 0), np.concatenate([fv1, fv1], 0)], 1).astype(np.float32).astype(NPBF)
    n1h = np.arange(64, dtype=np.float64)
    ps = TWO_PI * (k1[:, None, None] * n1h[None, None, :] / 128.0 + k1[:, None, None] * n2[None, :, None] / 8192.0)
    al = np.full(NK1, 2.0)
    al[0] = 1.0
    al[64] = 1.0
    sc = (al / 8192.0)[:, None, None]
    c["fe"] = np.stack([sc * np.cos(ps), -sc * np.sin(ps)], 2).astype(np.float32).astype(NPBF)
    n = np.arange(8192)
    m = np.where(n < L, n, 8192 - n)
    m[L] = 0
    t = np.linspace(0.0, 1.0, L, dtype=np.float32)
    w = (2.0 * math.pi * np.arange(L, dtype=np.float32) / L).astype(np.float32)
    f = np.linspace(1e-4, 15, 16, dtype=np.float32)
    fw = (f[None, :] * w[:, None]).astype(np.float32)
    z = np.concatenate([t[:, None], np.cos(fw), np.sin(fw)], -1).astype(np.float32)
    c["zt"] = np.ascontiguousarray(z[m].T)
    negt = -t[m]
    negt[L] = -1e4
    c["negt"] = np.ascontiguousarray(negt.reshape(128, 64).astype(np.float32))
    mind = math.log(1e-2) / 1.5
    maxd = math.log(1e-2) / 0.3
    dl = np.abs(np.linspace(mind, maxd, 512, dtype=np.float32))
    c["absd"] = np.ascontiguousarray(np.tile(dl[None, :], (128, 1)).astype(np.float32))
    return c


CONST_SPECS = [("ident", [128, 128], BF16), ("rope", [128, 32, 2, 8], F32), ("negm", [128, 2, 4, 128], BF16),
               ("fw1", [128, 2, NK1], BF16), ("fm2", [64, NK1, 2, 128], BF16), ("fv", [128, 2, 128], BF16),
               ("fe", [NK1, 64, 2, 64], BF16), ("zt", [33, 8192], F32), ("negt", [128, 64], F32),
               ("absd", [128, 512], F32)]

IN_SPECS = [("x", [NSEQ, L, D]), ("w_in", [D, D_IN]), ("g1r", [128, D]), ("shortwp", [128, 12, 3]),
            ("shortbp", [128, 12]), ("fw0", [33, 64]), ("fb0", [64, 1]), ("fwi", [2, 64, 64]), ("fbi", [64, 2]),
            ("ffreq", [64, 1]), ("fwo", [64, 2048]), ("hbias", [1, 1024]), ("sinkr", [128, 8]),
            ("w_up_h", [DH, D]), ("w_up_a", [DH, D]), ("w_o", [D, D]), ("g2r", [128, D]),
            ("w_gate", [D, D_FF]), ("w_up", [D, D_FF]), ("w_down", [D_FF, D]), ("gfr", [128, D])]


def build(debug=False):
    SK = "ExternalOutput" if debug else "Internal"
    nc = bass.Bass("TRN2", target_bir_lowering=False)
    es = ExitStack()
    S = Sched(nc, es)
    I = {}
    for name, shp in IN_SPECS:
        I[name] = nc.dram_tensor(name, list(shp), F32, kind="ExternalInput").ap()
    for name, shp, dt in CONST_SPECS:
        I[name] = nc.dram_tensor(name, list(shp), dt, kind="ExternalInput").ap()
    yout = nc.dram_tensor("y", [NSEQ, L, D], F32, kind="ExternalOutput").ap()
    PROJ = nc.dram_tensor("proj_s", [NSEQ, L, 2304], BF16, kind=SK).ap()
    SG = nc.dram_tensor("sg_s", [NSEQ, 2048, L], BF16, kind=SK).ap()
    YAT = nc.dram_tensor("yat_s", [NSEQ, DH, L], BF16, kind=SK).ap()
    YHT = nc.dram_tensor("yht_s", [NSEQ, DH, L], BF16, kind=SK).ap()
    X1 = nc.dram_tensor("x1_s", [NSEQ * L, D], F32, kind=SK).ap()
    KH = nc.dram_tensor("kh_s", [2, NCG, 128, NK1 * CG], BF16, kind=SK).ap()
    tk_proj, tk_sg, tk_yat, tk_yht, tk_x1, tk_kh = Tk(), Tk(), Tk(), Tk(), Tk(), Tk()
    tk_in = Tk()
    tk_out = Tk()

    uid = {"n": 0}

    def sb(st, name, shape, dt):
        uid["n"] += 1
        return st.enter_context(nc.sbuf_tensor("sb%d_%s" % (uid["n"], name), list(shape), dt))

    banks = []
    for b in range(8):
        banks.append((es.enter_context(nc.psum_tensor("ps%d" % b, [128, 512], F32)), Tk()))
    bstate = {"i": 0}

    def psum(pool=None):
        if pool is None:
            b = banks[(0, 1, 2, 3, 5, 6, 7)[bstate["i"] % 7]]
            bstate["i"] += 1
            return b
        lo, n = pool
        k = "p%d_%d" % (lo, n)
        bstate[k] = bstate.get(k, -1) + 1
        return banks[lo + bstate[k] % n]

    class Ring:
        def __init__(self, st, name, shape, dt, n):
            self.t = [(sb(st, "%s%d" % (name, i), shape, dt), Tk()) for i in range(n)]
            self.i = 0

        def next(self):
            r = self.t[self.i % len(self.t)]
            self.i += 1
            return r

    ident = sb(es, "ident", [128, 128], BF16)
    tk_c = Tk()
    S.dma("sp", ident[:], I["ident"][:, :], reads=[tk_in], writes=[tk_c])

    def mm(out, lhsT, rhs, start, stop, reads, writes, sig):
        S.op("pe", lambda e: e.matmul(out, lhsT, rhs, start=start, stop=stop), reads=reads, writes=writes, sig=sig)

    def norm_load(st_ring_x, srcs, q="sp"):
        xts = [st_ring_x.next() for _ in range(len(srcs))]
        for i in range(len(srcs)):
            S.dma(q, xts[i][0][:], srcs[i], reads=[tk_in, tk_x1], writes=[xts[i][1]])
        return xts

    def norm_group_to_T(st_ring_x, xn_ring, small_ring, srcs, dstT, dstT_tk, col0s, junk, junk_tk, grow, grow_tk, xts=None,
                        phase="both", xns=None):
        n = len(srcs)
        if phase == "trans":
            return norm_trans(xns, dstT, dstT_tk, col0s)
        if xts is None:
            xts = norm_load(st_ring_x, srcs)
        sms = [small_ring.next() for _ in range(n)]
        xns = [xn_ring.next() for _ in range(n)]
        for i in range(n):
            (xt, xtk), (sm, smtk) = xts[i], sms[i]
            S.op("act", lambda e: e.activation(out=junk[:], in_=xt[:], func=AF.Square, accum_out=sm[:, 0:1]),
                 reads=[xtk], writes=[junk_tk, smtk])
        for i in range(n):
            sm, smtk = sms[i]
            S.op("dve", lambda e: e.tensor_scalar(sm[:, 1:2], sm[:, 0:1], 1.0 / D, EPS, op0=ALU.mult, op1=ALU.add),
                 reads=[smtk], writes=[smtk])
        for i in range(n):
            sm, smtk = sms[i]
            S.op("act", lambda e: e.activation(out=sm[:, 2:3], in_=sm[:, 1:2], func=AF.Sqrt), reads=[smtk], writes=[smtk])
        for i in range(n):
            sm, smtk = sms[i]
            S.op("dve", lambda e: e.reciprocal(sm[:, 3:4], sm[:, 2:3]), reads=[smtk], writes=[smtk])
        for i in range(n):
            (xt, xtk), (sm, smtk), (xn, xntk) = xts[i], sms[i], xns[i]
            S.op("dve", lambda e: e.scalar_tensor_tensor(out=xn[:], in0=xt[:], scalar=sm[:, 3:4], in1=grow[:], op0=ALU.mult,
                                                         op1=ALU.mult), reads=[xtk, smtk, grow_tk], writes=[xntk])
        if phase == "stats":
            return xts, xns
        norm_trans(xns, dstT, dstT_tk, col0s)
        return xts

    def norm_trans(xns, dstT, dstT_tk, col0s):
        for i in range(len(xns)):
            xn, xntk = xns[i]
            bank, btk = psum()
            bv = bank[:].bitcast(BF16)
            for kc in range(8):
                S.op("pe", lambda e: e.transpose(bv[:, kc * 128:(kc + 1) * 128], xn[:, kc * 128:(kc + 1) * 128], ident[:]),
                     reads=[xntk, tk_c], writes=[btk], sig=(kc == 7))
            if i % 2 == 0:
                S.op("act", lambda e: e.copy(dstT[:, :, col0s[i]:col0s[i] + 128], bv.rearrange("p (k t) -> p k t", k=8)),
                     reads=[btk], writes=[dstT_tk])
            else:
                S.op("dve", lambda e: e.tensor_copy(dstT[:, :, col0s[i]:col0s[i] + 128], bv.rearrange("p (k t) -> p k t", k=8)),
                     reads=[btk], writes=[dstT_tk])

    FC = {}
    tk_f = Tk()

    def load_fft_consts(st):
        FC["fw1"] = sb(st, "fw1", [128, 2 * NK1], BF16)
        FC["fm2"] = sb(st, "fm2", [64, NK1, 2, 128], BF16)
        FC["fv"] = sb(st, "fv", [128, 2, 128], BF16)
        FC["fe"] = sb(st, "fe", [NK1, 64, 2, 64], BF16)
        S.dma("sp", FC["fw1"][:], I["fw1"].rearrange("p a k -> p (a k)"), reads=[tk_in], writes=[tk_f])
        S.dma("sp", FC["fm2"][:], I["fm2"][:, :, :, :], reads=[tk_in], writes=[tk_f])
        S.dma("sp", FC["fv"][:], I["fv"][:, :, :], reads=[tk_in], writes=[tk_f])
        S.dma("sp", FC["fe"][:], I["fe"][:, :, :, :], reads=[tk_in], writes=[tk_f])

    def fft_forward(st, xin, xin_tk, krows, YP, YP_tk, consume, alt=False, ywaits=(), part="both"):
        fw1, fm2 = FC["fw1"], FC["fm2"]
        Yv = YP[0:64, :].rearrange("p (q c) -> p q c", c=CG)
        for c0 in (range(0, CG, 3) if part in ("both", "s1") else ()):
            nch = min(3, CG - c0)
            bank, btk = psum()
            for cc in range(nch):
                mm(bank[0:64, cc * 130:(cc + 1) * 130], xin[0:krows, :, c0 + cc], fw1[0:krows, :], True, True,
                   [xin_tk, tk_f], [btk], cc == nch - 1)
            if alt and (c0 // 3) % 2 == 1:
                S.op("dve", lambda e: e.tensor_copy(Yv[:, :, c0:c0 + nch],
                                                    bank[0:64, 0:nch * 130].rearrange("p (c q) -> p q c", c=nch)),
                     reads=[btk], writes=[YP_tk], waits=ywaits)
            else:
                S.op("act", lambda e: e.copy(Yv[:, :, c0:c0 + nch],
                                             bank[0:64, 0:nch * 130].rearrange("p (c q) -> p q c", c=nch)),
                     reads=[btk], writes=[YP_tk], waits=ywaits)
        for k1a in (range(0, NK1, 8) if part in ("both", "s2") else ()):
            nk = min(8, NK1 - k1a)
            bank, btk = psum()
            for kk in range(nk):
                k1 = k1a + kk
                o_x = bank[:, kk * CG:(kk + 1) * CG]
                mm(o_x, fm2[:, k1, 0, :], Yv[:, k1, :], True, False, [YP_tk, tk_f], [btk], False)
                mm(o_x, fm2[:, k1, 1, :], Yv[:, NK1 + k1, :], False, True, [YP_tk, tk_f], [btk], kk == nk - 1)
            consume(bank, btk, k1a, nk)

    with ExitStack() as st:
        load_fft_consts(st)
        wo = sb(st, "fwo", [64, 2048], BF16)
        fpar = sb(st, "fpar", [64, 8], F32)
        negt = sb(st, "negt", [128, 64], F32)
        absd = sb(st, "absd", [128, 512], F32)
        hb = sb(st, "hb", [1, 1024], F32)
        h3 = sb(st, "h3", [64, 8192], BF16)
        st_mlp = ExitStack()
        zt = sb(st_mlp, "zt", [33, 8192], F32)
        w0 = sb(st_mlp, "fw0", [33, 64], F32)
        wi = sb(st_mlp, "fwi", [64, 2, 64], F32)
        tk_fp = Tk()
        tk_h3 = Tk()
        S.dma("sp", zt[:], I["zt"][:, :], reads=[tk_in], writes=[tk_fp])
        S.dma("sp", w0[:], I["fw0"][:, :], reads=[tk_in], writes=[tk_fp])
        S.dma("sp", wi[:], I["fwi"].rearrange("a i o -> i a o"), reads=[tk_in], writes=[tk_fp])
        S.dma("pool", wo[:], I["fwo"][:, :], reads=[tk_in], writes=[tk_fp])
        S.dma("sp", fpar[:, 0:1], I["ffreq"][:, :], reads=[tk_in], writes=[tk_fp])
        S.dma("sp", fpar[:, 1:2], I["fb0"][:, :], reads=[tk_in], writes=[tk_fp])
        S.dma("sp", fpar[:, 2:4], I["fbi"][:, :], reads=[tk_in], writes=[tk_fp])
        S.dma("sp", negt[:], I["negt"][:, :], reads=[tk_in], writes=[tk_fp])
        S.dma("sp", absd[:], I["absd"][:, :], reads=[tk_in], writes=[tk_fp])
        S.dma("sp", hb[:], I["hbias"][:, :], reads=[tk_in], writes=[tk_fp])
        S.op("dve", lambda e: e.tensor_scalar(fpar[:, 4:7], fpar[:, 1:4], fpar[:, 0:1], None, op0=ALU.mult),
             reads=[tk_fp], writes=[tk_fp])
        hring = Ring(st_mlp, "fh", [64, 512], F32, 8)
        tring = Ring(st_mlp, "ft", [64, 512], F32, 10)
        for pg in range(4):
            curs = [None] * 4
            for layer in range(3):
                for q4 in range(4):
                    pc = pg * 4 + q4
                    cur = curs[q4]
                    bank, btk = psum()
                    if layer == 0:
                        mm(bank[0:64, :], w0[:, :], zt[:, pc * 512:(pc + 1) * 512], True, True, [tk_fp], [btk], True)
                    else:
                        mm(bank[0:64, :], wi[:, layer - 1, :], cur[0][:], True, True, [tk_fp, cur[1]], [btk], True)
                    t1, t1k = tring.next()
                    S.op("dve", lambda e: e.tensor_scalar(t1[:], bank[0:64, :], fpar[:, 0:1], fpar[:, 4 + layer:5 + layer],
                                                          op0=ALU.mult, op1=ALU.add), reads=[btk, tk_fp], writes=[t1k])
                    t2, t2k = tring.next()
                    S.op("dve", lambda e: e.tensor_scalar(t2[:], t1[:], 1.0 / TWO_PI, MAGIC, op0=ALU.mult, op1=ALU.add),
                         reads=[t1k], writes=[t2k])
                    S.op("dve", lambda e: e.tensor_scalar(t2[:], t2[:], MAGIC, -TWO_PI, op0=ALU.subtract, op1=ALU.mult),
                         reads=[t2k], writes=[t2k])
                    S.op("dve", lambda e: e.tensor_tensor(t1[:], t1[:], t2[:], op=ALU.add), reads=[t1k, t2k], writes=[t1k])
                    if layer < 2:
                        hn, hnk = hring.next()
                        S.op("act", lambda e: e.activation(out=hn[:], in_=t1[:], func=AF.Sin), reads=[t1k], writes=[hnk])
                        curs[q4] = (hn, hnk)
                    else:
                        S.op("act", lambda e: e.activation(out=h3[:, pc * 512:(pc + 1) * 512], in_=t1[:], func=AF.Sin),
                             reads=[t1k], writes=[tk_h3])
        S.barrier()
        st_mlp.close()
        h3v = h3[:].rearrange("p (a b) -> p a b", b=64)
        kt = sb(st, "kt", [128, 64, 512], BF16)
        tk_kt = Tk()
        YPfs = [(sb(st, "ypf%d" % i, [128, 130 * CG], BF16), Tk()) for i in range(2)]
        dring = Ring(st, "fdec", [128, 512], F32, 2)
        xsring = Ring(st, "fxs", [128, 8 * CG], BF16, 3)
        for o in range(2):
            colf = (0 * 2 + o) * 512
            colb = (1 * 2 + o) * 512
            for n2 in range(64):
                dec, deck = dring.next()
                S.op("act", lambda e: e.activation(out=dec[:], in_=absd[:], func=AF.Exp, scale=negt[:, n2:n2 + 1]),
                     reads=[tk_fp], writes=[deck])
                bf_, bfk = psum()
                bb_, bbk = psum()
                mm(bf_[:, :], h3v[:, :, n2], wo[:, colf:colf + 512], True, True, [tk_h3, tk_fp], [bfk], True)
                mm(bb_[:, :], h3v[:, :, n2], wo[:, colb:colb + 512], True, True, [tk_h3, tk_fp], [bbk], True)
                S.op("dve", lambda e: e.tensor_tensor(kt[0:64, n2, :], bf_[0:64, :], dec[0:64, :], op=ALU.mult),
                     reads=[bfk, deck], writes=[tk_kt])
                S.op("dve", lambda e: e.tensor_tensor(kt[64:128, n2, :], bb_[64:128, :], dec[64:128, :], op=ALU.mult),
                     reads=[bbk, deck], writes=[tk_kt])
            S.op("dve", lambda e: e.tensor_tensor(kt[0:1, 0, :], kt[0:1, 0, :], hb[0:1, o * 512:(o + 1) * 512], op=ALU.add),
                 reads=[tk_kt, tk_fp], writes=[tk_kt])
            for cg in range(NCG):
                def consume(bank, btk, k1a, nk, o=o, cg=cg):
                    xs, xsk = xsring.next()
                    if (k1a // 8) % 2 == 0:
                        S.op("dve", lambda e: e.tensor_copy(xs[:, 0:nk * CG], bank[:, 0:nk * CG]), reads=[btk], writes=[xsk])
                    else:
                        S.op("act", lambda e: e.copy(xs[:, 0:nk * CG], bank[:, 0:nk * CG]), reads=[btk], writes=[xsk])
                    S.dma("sp", KH[o, cg, :, k1a * CG:(k1a + nk) * CG], xs[:, 0:nk * CG], reads=[xsk], writes=[tk_kh])

                fft_forward(st, kt[:, :, cg * CG:(cg + 1) * CG], tk_kt, 128, YPfs[cg % 2][0], YPfs[cg % 2][1], consume, alt=True)
        S.barrier()

    for _pass in (0,):
        with ExitStack() as st:
            wq = sb(st, "wq", [128, 8, 2816], BF16)
            wh = sb(st, "wh", [128, 8, 1536], BF16)
            g1 = sb(st, "g1", [128, D], F32)
            shwp = sb(st, "shwp", [128, 12, 3], F32)
            shbp = sb(st, "shbp", [128, 12], F32)
            tk_w = Tk()
            S.dma("sp", g1[:], I["g1r"][:, :], reads=[tk_in], writes=[tk_w])
            S.dma("sp", shwp[:], I["shortwp"][:, :, :], reads=[tk_in], writes=[tk_w])
            S.dma("sp", shbp[:], I["shortbp"][:, :], reads=[tk_in], writes=[tk_w])
            w_in_v = I["w_in"].rearrange("(k p) c -> p k c", p=128)
            for kc in range(8):
                S.dma("pool", wq[:, kc, :], w_in_v[:, kc, 1536:D_IN], reads=[tk_in], writes=[tk_w])
                S.dma("pool", wh[:, kc, :], w_in_v[:, kc, 0:1536], reads=[tk_in], writes=[tk_w])
            hT = [(sb(st, "hT%d" % i, [128, 8, 514], BF16), Tk()) for i in range(3)]
            for i in range(3):
                S.op("pool", lambda e: e.memset(hT[i][0][:], 0.0), writes=[hT[i][1]])
            xr = Ring(st, "xr", [128, D], F32, 8)
            xnr = Ring(st, "xnr", [128, D], BF16, 4)
            smr = Ring(st, "smr", [128, 4], F32, 8)
            junk = sb(st, "junk", [128, D], BF16)
            junk_tk = Tk()
            stg = Ring(st, "stg", [128, 4, 2304], BF16, 2)
            sgr = Ring(st, "sgr", [128, 512], BF16, 4)
            uer = Ring(st, "ue", [128, 514], F32, 3)
            ocr = Ring(st, "oc", [128, 512], F32, 3)
            obr = Ring(st, "ob", [128, 512], BF16, 3)

            for s in range(NSEQ):
                def xsrcs(g):
                    return [I["x"][s, g * 512 + m * 128:g * 512 + (m + 1) * 128, :] for m in range(4)]

                def normgroup(g, xts, phase="both", xns=None):
                    buf_, btk2_ = hT[g % 3]
                    return norm_group_to_T(xr, xnr, smr, xsrcs(g), buf_[:, :, 1:513], btk2_, [m * 128 for m in range(4)], junk,
                                           junk_tk, g1, tk_w, xts=xts, phase=phase, xns=xns)

                def halos(g):
                    b0, k0 = hT[g % 3]
                    if g == 0:
                        S.op("pool", lambda e: e.memset(b0[:, :, 0:1], 0.0), writes=[k0])
                    if g + 1 < 8:
                        b1, k1_ = hT[(g + 1) % 3]
                        S.op("pool", lambda e: e.tensor_copy(b0[:, :, 513:514], b1[:, :, 1:2]), reads=[k1_], writes=[k0])
                        S.op("pool", lambda e: e.tensor_copy(b1[:, :, 0:1], b0[:, :, 512:513]), reads=[k0], writes=[k1_])
                    else:
                        S.op("pool", lambda e: e.memset(b0[:, :, 513:514], 0.0), writes=[k0])

                xpre = {0: norm_load(xr, xsrcs(0), "pool"), 1: norm_load(xr, xsrcs(1), "pool")}
                normgroup(0, xpre.pop(0))
                normgroup(1, xpre.pop(1))
                halos(0)
                for g in range(8):
                    buf, btk_ = hT[g % 3]
                    if g + 2 < 8:
                        xpre[g + 2] = norm_load(xr, xsrcs(g + 2), "pool")
                    so, sok = stg.next()
                    def hy_u(ct):
                        bu, buk = psum((0, 4))
                        for kc in range(8):
                            mm(bu[:, :], wh[:, kc, ct * 128:(ct + 1) * 128], buf[:, kc, 1:513], kc == 0, kc == 7, [btk_, tk_w], [buk], kc == 7)
                        return bu, buk

                    def hy_h(ct):
                        bh, bhk = banks[4][0][:, (ct % 2) * 2:(ct % 2) * 2 + 2], banks[4][1]
                        for kc in range(8):
                            mm(bh, wh[:, kc, ct * 128:(ct + 1) * 128], buf[:, kc, 0:514:513], kc == 0, kc == 7, [btk_, tk_w], [bhk], kc == 7)
                        return bh, bhk

                    def hy_ew(ct, und, hnd):
                        bu, buk = und
                        bh, bhk = hnd
                        ue, uek = uer.next()
                        S.op("act", lambda e: e.copy(ue[:, 1:513], bu[:, :]), reads=[buk], writes=[uek])
                        S.op("act", lambda e: e.copy(ue[:, 0:514:513], bh), reads=[bhk], writes=[uek])
                        oc, ock = ocr.next()
                        S.op("act", lambda e: e.activation(out=oc[:], in_=ue[:, 1:513], func=AF.Identity, scale=shwp[:, ct, 1:2],
                                                           bias=shbp[:, ct:ct + 1]), reads=[uek, tk_w], writes=[ock])
                        S.op("dve", lambda e: e.scalar_tensor_tensor(out=oc[:], in0=ue[:, 0:512], scalar=shwp[:, ct, 0:1], in1=oc[:],
                                                                     op0=ALU.mult, op1=ALU.add), reads=[uek, tk_w, ock], writes=[ock])
                        ob_, obk_ = obr.next()
                        S.op("dve", lambda e: e.scalar_tensor_tensor(out=ob_[:], in0=ue[:, 2:514], scalar=shwp[:, ct, 2:3], in1=oc[:],
                                                                     op0=ALU.mult, op1=ALU.add), reads=[uek, tk_w, ock], writes=[obk_])
                        return ob_, obk_

                    def hy_tr(ct, ob_, obk_):
                        bt, btk2 = psum((5, 3))
                        bv = bt[:].bitcast(BF16)
                        for m in range(4):
                            S.op("pe", lambda e: e.transpose(bv[:, m * 128:(m + 1) * 128], ob_[:, m * 128:(m + 1) * 128], ident[:]),
                                 reads=[obk_, tk_c], writes=[btk2], sig=(m == 3))
                        S.op("dve", lambda e: e.tensor_copy(so[:, :, ct * 128:(ct + 1) * 128], bv[:, 0:512].rearrange("p (m c) -> p m c", m=4)),
                             reads=[btk2], writes=[sok])

                    pend = [hy_u(0), hy_u(1)]
                    hcur = hy_h(0)
                    nst = None
                    for ct in range(12):
                        if ct + 2 < 12:
                            pend.append(hy_u(ct + 2))
                        ob_, obk_ = hy_ew(ct, pend.pop(0), hcur)
                        if ct + 1 < 12:
                            hcur = hy_h(ct + 1)
                        hy_tr(ct, ob_, obk_)
                    nst = None
                    if g + 2 < 8:
                        nst = normgroup(g + 2, xpre.pop(g + 2), phase="stats")
                    for m in range(4):
                        c1 = 1 + m * 128
                        for (c0, cw) in ((0, 512), (512, 256)):
                            bank, bk = psum()
                            for kc in range(8):
                                mm(bank[:, 0:cw], buf[:, kc, c1:c1 + 128], wq[:, kc, c0:c0 + cw], kc == 0, kc == 7,
                                   [btk_, tk_w], [bk], kc == 7)
                            S.op("act", lambda e: e.copy(so[:, m, 1536 + c0:1536 + c0 + cw], bank[:, 0:cw]), reads=[bk],
                                 writes=[sok])
                    S.dma("sp", PROJ[s, g * 512:(g + 1) * 512, :].rearrange("(m p) c -> p m c", p=128), so[:], reads=[sok],
                          writes=[tk_proj])
                    for gc in range(16):
                        if gc == 3:
                            if g + 2 < 8:
                                normgroup(g + 2, None, phase="trans", xns=nst[1])
                            if g + 1 < 8:
                                halos(g + 1)
                        bank, bk = psum()
                        for kc in range(8):
                            mm(bank[:, :], wq[:, kc, 768 + gc * 128:768 + (gc + 1) * 128], buf[:, kc, 1:513], kc == 0, kc == 7,
                               [btk_, tk_w], [bk], kc == 7)
                        sg_, sgk = sgr.next()
                        S.op("act", lambda e: e.activation(out=sg_[:], in_=bank[:, :], func=AF.Sigmoid), reads=[bk],
                             writes=[sgk])
                        S.dma("sp", SG[s, gc * 128:(gc + 1) * 128, g * 512:(g + 1) * 512], sg_[:], reads=[sgk], writes=[tk_sg])
            S.barrier()

        with ExitStack() as st:
            QT = sb(st, "QT", [128, 32, 4, 128], BF16)
            KT = sb(st, "KT", [128, L], BF16)
            VX = sb(st, "VX", [128, 32, 2, 65], BF16)
            rope = sb(st, "rope", [128, 32, 2, 8], F32)
            negm = sb(st, "negm", [128, 2, 4, 128], BF16)
            esk = sb(st, "esk", [128, 8], F32)
            tk_ac = Tk()
            tk_qt, tk_kt2, tk_vx = Tk(), Tk(), Tk()
            S.dma("sp", rope[:], I["rope"][:, :, :, :], reads=[tk_in], writes=[tk_ac])
            S.dma("sp", negm[:], I["negm"][:, :, :, :], reads=[tk_in], writes=[tk_ac])
            S.dma("sp", esk[:], I["sinkr"][:, :], reads=[tk_in], writes=[tk_ac])
            S.op("act", lambda e: e.activation(out=esk[:], in_=esk[:], func=AF.Exp), reads=[tk_ac], writes=[tk_ac])
            S.op("pool", lambda e: e.memset(VX[:], 1.0), writes=[tk_vx])
            qr = Ring(st, "qkv", [128, 768], BF16, 3)
            qfr = Ring(st, "qkf", [128, 10, 16], F32, 2)
            qpr = Ring(st, "qp", [128, 512], BF16, 2)
            rtr = Ring(st, "rt", [128, 4, 10, 8], F32, 2)
            for s in range(NSEQ):
                for m in range(32):
                    qt_, qk = qr.next()
                    S.dma("sp", qt_[:], PROJ[s, m * 128:(m + 1) * 128, 1536:2304], reads=[tk_proj], writes=[qk])
                    qv = qt_[:, 0:640].rearrange("p (h d) -> p h d", d=64)
                    qf, qfk = qfr.next()
                    S.op("dve", lambda e: e.tensor_copy(qf[:], qv[:, :, 0:16]), reads=[qk], writes=[qfk])
                    rt, rtk = rtr.next()
                    cosb = rope[:, m, 0:1, :].broadcast_to([128, 10, 8])
                    sinb = rope[:, m, 1:2, :].broadcast_to([128, 10, 8])
                    a_ = qf[:, :, 0:8]
                    b_ = qf[:, :, 8:16]
                    S.op("dve", lambda e: e.tensor_tensor(rt[:, 0, :, :], a_, cosb, op=ALU.mult), reads=[qfk, tk_ac], writes=[rtk])
                    S.op("dve", lambda e: e.tensor_tensor(rt[:, 1, :, :], b_, sinb, op=ALU.mult), reads=[qfk, tk_ac], writes=[rtk])
                    S.op("dve", lambda e: e.tensor_tensor(rt[:, 2, :, :], b_, cosb, op=ALU.mult), reads=[qfk, tk_ac], writes=[rtk])
                    S.op("dve", lambda e: e.tensor_tensor(rt[:, 3, :, :], a_, sinb, op=ALU.mult), reads=[qfk, tk_ac], writes=[rtk])
                    S.op("dve", lambda e: e.tensor_tensor(qv[:, :, 0:8], rt[:, 0, :, :], rt[:, 1, :, :], op=ALU.subtract),
                         reads=[rtk], writes=[qk])
                    S.op("dve", lambda e: e.tensor_tensor(qv[:, :, 8:16], rt[:, 2, :, :], rt[:, 3, :, :], op=ALU.add),
                         reads=[rtk], writes=[qk])
                    bank, bk = psum()
                    bv = bank[:].bitcast(BF16)
                    qp, qpk = qpr.next()
                    S.op("act", lambda e: e.copy(qp[:].rearrange("p (j g d) -> p j g d", j=4, g=2),
                                                 qt_[:, 0:512].rearrange("p (g j d) -> p j g d", g=2, j=4)),
                         reads=[qk], writes=[qpk])
                    for j in range(4):
                        S.op("pe", lambda e: e.transpose(bv[:, j * 128:(j + 1) * 128], qp[:, j * 128:(j + 1) * 128], ident[:]),
                             reads=[qpk, tk_c], writes=[bk], sig=False)
                    S.op("pe", lambda e: e.transpose(bv[:, 512:640], qt_[:, 512:640], ident[:]), reads=[qk, tk_c], writes=[bk])
                    S.op("act", lambda e: e.copy(QT[:, m, :, :].rearrange("p j t -> p (j t)"), bv[:, 0:512]),
                         reads=[bk], writes=[tk_qt])
                    S.op("act", lambda e: e.copy(KT[:, m * 128:(m + 1) * 128], bv[:, 512:640]), reads=[bk], writes=[tk_kt2])
                    S.op("pool", lambda e: e.tensor_copy(VX[:, m, :, 0:64], qt_[:, 640:768].rearrange("p (g d) -> p g d", g=2)),
                         reads=[qk], writes=[tk_vx])
                ptr = Ring(st, "pt", [128, 512], BF16, 8)
                yar = Ring(st, "ya", [128, 512], BF16, 2)
                dnr = Ring(st, "dn", [128, 8], F32, 2)
                yatg = Ring(st, "yatg", [128, 4, 512], BF16, 2)
                SP, OP = (0, 6), (6, 2)

                def s_step(i, gk):
                    pr = slice(gk * 64, gk * 64 + 64)
                    chunks = [c for c in (-1, 0, 1) if 0 <= i + c < 32]
                    pts = []
                    for c in chunks:
                        sbank, sbk = psum(SP)
                        kt0 = (i + c) * 128
                        mm(sbank[:, :], KT[pr, kt0:kt0 + 128], QT[pr, i, :, :].rearrange("p j t -> p (j t)"), True, c == 0,
                           [tk_kt2, tk_qt], [sbk], c == 0)
                        if c != 0:
                            mm(sbank[:, :], ident[:], negm[:, 0 if c < 0 else 1, :, :].rearrange("p j t -> p (j t)"), False, True,
                               [tk_c, tk_ac], [sbk], True)
                        pt, ptk = ptr.next()
                        S.op("act", lambda e: e.activation(out=pt[:], in_=sbank[:, :], func=AF.Exp, scale=0.125),
                             reads=[sbk], writes=[ptk])
                        pts.append((c, pt, ptk))
                    return pts

                def pv_step(i, gk, pts, ya, yak, dn, dnk):
                    ob, obk = psum(OP)
                    for j in range(4):
                        for ci, (c, pt, ptk) in enumerate(pts):
                            mm(ob[:, j * 128:j * 128 + 65], pt[:, j * 128:(j + 1) * 128], VX[:, i + c, gk, :],
                               ci == 0, ci == len(pts) - 1, [ptk, tk_vx], [obk], (j == 3) and (ci == len(pts) - 1))
                    ov = ob[:, :].rearrange("p (h d) -> p h d", d=128)
                    S.op("dve", lambda e: e.tensor_tensor(dn[:, gk * 4:gk * 4 + 4], ov[:, :, 64], esk[:, gk * 4:gk * 4 + 4], op=ALU.add),
                         reads=[obk, tk_ac], writes=[dnk])
                    S.op("dve", lambda e: e.reciprocal(dn[:, gk * 4:gk * 4 + 4], dn[:, gk * 4:gk * 4 + 4]), reads=[dnk], writes=[dnk])
                    S.op("dve", lambda e: e.tensor_tensor(
                        ya[:, gk * 256:(gk + 1) * 256].rearrange("p (h d) -> p h d", d=64), ov[:, :, 0:64],
                        dn[:, gk * 4:gk * 4 + 4].unsqueeze(2).broadcast_to([128, 4, 64]), op=ALU.mult),
                         reads=[obk, dnk], writes=[yak])

                work = [(i, gk) for i in range(32) for gk in range(2)]
                pend = s_step(*work[0])
                cur_ya = None
                yg = ygk = None
                for w, (i, gk) in enumerate(work):
                    nxt = s_step(*work[w + 1]) if w + 1 < len(work) else None
                    if gk == 0:
                        cur_ya = yar.next() + dnr.next()
                        if i % 4 == 0:
                            yg, ygk = yatg.next()
                    ya, yak, dn, dnk = cur_ya
                    pv_step(i, gk, pend, ya, yak, dn, dnk)
                    pend = nxt
                    if gk == 1:
                        bank, bk = psum(OP)
                        bv = bank[:].bitcast(BF16)
                        for kc in range(4):
                            S.op("pe", lambda e: e.transpose(bv[:, kc * 128:(kc + 1) * 128], ya[:, kc * 128:(kc + 1) * 128], ident[:]),
                                 reads=[yak, tk_c], writes=[bk], sig=(kc == 3))
                        mq = i % 4
                        S.op("act", lambda e: e.copy(yg[:, :, mq * 128:(mq + 1) * 128], bv[:, 0:512].rearrange("p (k t) -> p k t", k=4)),
                             reads=[bk], writes=[ygk])
                        if mq == 3:
                            g = i // 4
                            S.dma("sp", YAT[s].rearrange("(k p) t -> p k t", p=128)[:, :, g * 512:(g + 1) * 512], yg[:], reads=[ygk],
                                  writes=[tk_yat])
            S.barrier()

        with ExitStack() as st:
            load_fft_consts(st)
            fv, fe = FC["fv"], FC["fe"]
            tvs = [(sb(st, "tv%d" % i, [64, 64, CG], BF16), Tk()) for i in range(2)]
            gater = Ring(st, "gate", [64, 64, CG], BF16, 2)
            YPs = [(sb(st, "YP%d" % i, [128, 130 * CG], BF16), Tk()) for i in range(2)]
            ABk = [(Tk(), Tk()) for i in range(2)]
            XS = sb(st, "XS", [128, 72, CG], BF16)
            tk_xs = Tk()
            DDT = sb(st, "DDT", [NK1, 2 * 64 * CG], BF16)
            DD = DDT[:, :].rearrange("p (r n c) -> p r n c", r=2, n=64)
            tk_dd = Tk()
            khr = Ring(st, "khb", [128, 2, NK1, CG], BF16, 2)
            yT = sb(st, "yT", [64, L], BF16)
            tk_yT = Tk()

            def half1(step):
                s_, cg, o, slot = step["s"], step["cg"], step["o"], step["slot"]
                tv, tkv = tvs[slot]
                PRv = PROJ[s_].rearrange("(a b) c -> a b c", b=64)
                if o == 0:
                    S.dma("sp", tv[:], PRv[:, :, cg * CG:(cg + 1) * CG], reads=[tk_proj], writes=[tkv])
                gate, gate_tk = gater.next()
                gc0 = (512 if o == 0 else 1024) + cg * CG
                S.dma("sp", gate[:], PRv[:, :, gc0:gc0 + CG], reads=[tk_proj], writes=[gate_tk])
                kh, khk = khr.next()
                S.dma("sp", kh[:, 0, :, :].rearrange("p a c -> p (a c)"), KH[o, cg, :, :], reads=[tk_kh], writes=[khk])
                S.dma("sp", kh[0:64, 1, :, :].rearrange("p a c -> p (a c)"), KH[o, cg, 64:128, :], reads=[tk_kh], writes=[khk])
                S.dma("sp", kh[64:128, 1, :, :].rearrange("p a c -> p (a c)"), KH[o, cg, 0:64, :], reads=[tk_kh], writes=[khk])
                step.update(gate=gate, gate_tk=gate_tk, kh=kh, khk=khk)
                YP, tk_yp = YPs[step["idx"] % 2]
                tk_a, tk_b = ABk[step["idx"] % 2]
                step.update(YP=YP, tk_yp=tk_yp, tk_a=tk_a, tk_b=tk_b)

                fft_forward(st, tv, tkv, 64, YP, tk_yp, None, ywaits=[tk_a, tk_b], part="s1")

            def half1b(step):
                tv, tkv = tvs[step["slot"]]

                def consume(bank, btk, k1a, nk):
                    S.op("act", lambda e: e.copy(XS[:, k1a:k1a + nk, :].rearrange("p a c -> p (a c)"),
                                                 bank[:, 0:nk * CG]), reads=[btk], writes=[tk_xs])

                fft_forward(st, tv, tkv, 64, step["YP"], step["tk_yp"], consume, part="s2")

            def half2a(step):
                s_, cg, o, slot = step["s"], step["cg"], step["o"], step["slot"]
                tv, tkv = tvs[slot]
                kh, khk, gate, gate_tk = step["kh"], step["khk"], step["gate"], step["gate_tk"]
                YP, tk_yp = step["YP"], step["tk_yp"]
                Pv = YP[:, 0:2 * CG * NK1].rearrange("p (r c k) -> p r c k", r=2, c=CG)
                Xs = XS[:, 0:NK1, :]
                tk_a, tk_b = step["tk_a"], step["tk_b"]
                S.op("dve", lambda e: e.tensor_tensor(Pv[:, 0, :, :].rearrange("p c k -> p k c"), Xs, kh[:, 0, :, :], op=ALU.mult),
                     reads=[tk_xs, khk], writes=[tk_a], waits=[tk_yp])
                S.op("pool", lambda e: e.tensor_tensor(Pv[:, 1, :, :].rearrange("p c k -> p k c"), Xs, kh[:, 1, :, :], op=ALU.mult),
                     reads=[tk_xs, khk], writes=[tk_b], waits=[tk_yp])

            def half2b(step):
                s_, cg, o, slot = step["s"], step["cg"], step["o"], step["slot"]
                tv, tkv = tvs[slot]
                gate, gate_tk = step["gate"], step["gate_tk"]
                YP, tk_yp = step["YP"], step["tk_yp"]
                Pv = YP[:, 0:2 * CG * NK1].rearrange("p (r c k) -> p r c k", r=2, c=CG)
                for c0 in range(0, CG, 4):
                    bank, bk = psum()
                    for cc in range(4):
                        c = c0 + cc
                        mm(bank[0:NK1, cc * 128:(cc + 1) * 128], Pv[:, 0, c, :], fv[:, 0, :], True, False, [step["tk_a"], tk_f], [bk], False)
                        mm(bank[0:NK1, cc * 128:(cc + 1) * 128], Pv[:, 1, c, :], fv[:, 1, :], False, True, [step["tk_b"], tk_f], [bk], cc == 3)
                    S.op("act", lambda e: e.copy(DD[:, :, :, c0:c0 + 4].rearrange("p r n c -> p (r n) c"),
                                                 bank[0:NK1, :].rearrange("p (c q) -> p q c", c=4)), reads=[bk], writes=[tk_dd])

            def half2c(step):
                s_, cg, o, slot = step["s"], step["cg"], step["o"], step["slot"]
                tv, tkv = tvs[slot]
                gate, gate_tk = step["gate"], step["gate_tk"]
                for n0 in range(0, 64, 8):
                    bank, bk = psum()
                    for nn in range(8):
                        n2 = n0 + nn
                        mm(bank[0:64, nn * CG:(nn + 1) * CG], fe[:, n2, 0, :], DD[:, 0, n2, :], True, False, [tk_dd, tk_f], [bk], False)
                        mm(bank[0:64, nn * CG:(nn + 1) * CG], fe[:, n2, 1, :], DD[:, 1, n2, :], False, True, [tk_dd, tk_f], [bk], nn == 7)
                    S.op("dve", lambda e: e.tensor_tensor(tv[:, n0:n0 + 8, :].rearrange("p a c -> p (a c)"), bank[0:64, :],
                                                          gate[:, n0:n0 + 8, :].rearrange("p a c -> p (a c)"), op=ALU.mult),
                         reads=[bk, gate_tk], writes=[tkv])
                if o == 1:
                    yTv = yT[:].rearrange("p (a b) -> p a b", b=64)
                    for n0 in range(0, 64, 16):
                        bank, bk = psum()
                        bv = bank[:].bitcast(BF16)
                        for nn in range(16):
                            S.op("pe", lambda e: e.transpose(bv[0:CG, nn * 64:(nn + 1) * 64], tv[:, n0 + nn, :], ident[0:64, 0:64]),
                                 reads=[tkv, tk_c], writes=[bk], sig=(nn == 15))
                        S.op("act", lambda e: e.copy(yTv[:, :, n0:n0 + 16], bv[0:CG, :].rearrange("p (b a) -> p a b", b=16)),
                             reads=[bk], writes=[tk_yT])
                    S.dma("sp", YHT[s_, cg * CG:(cg + 1) * CG, :], yT[:], reads=[tk_yT], writes=[tk_yht])

            steps = []
            for s in range(NSEQ):
                for pair in range(NCG // 2):
                    for o in range(2):
                        for slot in range(2):
                            steps.append(dict(s=s, cg=2 * pair + slot, o=o, slot=slot, idx=len(steps)))
            half1(steps[0])
            half1b(steps[0])
            for k in range(len(steps)):
                half2a(steps[k])
                if k + 1 < len(steps):
                    half1(steps[k + 1])
                half2b(steps[k])
                if k + 1 < len(steps):
                    half1b(steps[k + 1])
                half2c(steps[k])
            S.barrier()

        with ExitStack() as st:
            wuh = sb(st, "wuh", [128, 4, D], BF16)
            wua = sb(st, "wua", [128, 4, D], BF16)
            wo_ = sb(st, "wo_", [128, 8, D], BF16)
            tk_w4 = Tk()
            S.dma("pool", wuh[:], I["w_up_h"].rearrange("(k p) c -> p k c", p=128), reads=[tk_in], writes=[tk_w4])
            S.dma("pool", wua[:], I["w_up_a"].rearrange("(k p) c -> p k c", p=128), reads=[tk_in], writes=[tk_w4])
            S.dma("pool", wo_[:], I["w_o"].rearrange("(k p) c -> p k c", p=128), reads=[tk_in], writes=[tk_w4])
            yhr = Ring(st, "yhg", [128, 4, 512], BF16, 2)
            yagr = Ring(st, "yag", [128, 4, 512], BF16, 2)
            sggr = Ring(st, "sgg", [128, 16, 512], BF16, 2)
            mTr = Ring(st, "mT", [128, 8, 512], BF16, 2)
            t1r = Ring(st, "m1", [128, 512], F32, 2)
            t2r = Ring(st, "m2", [128, 512], F32, 2)
            xr = Ring(st, "x4", [128, D], F32, 8)
            x1r = Ring(st, "x14", [128, D], F32, 3)
            for s in range(NSEQ):
                def prefetch4(g):
                    ts = slice(g * 512, (g + 1) * 512)
                    yh, yhk = yhr.next()
                    yag, yagk = yagr.next()
                    sgg, sggk = sggr.next()
                    S.dma("sp", yh[:], YHT[s].rearrange("(k p) t -> p k t", p=128)[:, :, ts], reads=[tk_yht], writes=[yhk])
                    S.dma("sp", yag[:], YAT[s].rearrange("(k p) t -> p k t", p=128)[:, :, ts], reads=[tk_yat], writes=[yagk])
                    S.dma("sp", sgg[:], SG[s].rearrange("(k p) t -> p k t", p=128)[:, :, ts], reads=[tk_sg], writes=[sggk])
                    xts = []
                    for m in range(4):
                        xt, xtk = xr.next()
                        t0 = g * 512 + m * 128
                        S.dma("sp", xt[:], I["x"][s, t0:t0 + 128, :], reads=[tk_in], writes=[xtk])
                        xts.append((xt, xtk))
                    return (yh, yhk, yag, yagk, sgg, sggk, xts)

                nxt4 = prefetch4(0)
                for g in range(8):
                    yh, yhk, yag, yagk, sgg, sggk, xts = nxt4
                    if g + 1 < 8:
                        nxt4 = prefetch4(g + 1)
                    mT, mTk = mTr.next()
                    for mc in range(8):
                        bh, bhk = psum()
                        ba, bak = psum()
                        for kc in range(4):
                            mm(bh[:, :], wuh[:, kc, mc * 128:(mc + 1) * 128], yh[:, kc, :], kc == 0, kc == 3, [tk_w4, yhk], [bhk], kc == 3)
                        for kc in range(4):
                            mm(ba[:, :], wua[:, kc, mc * 128:(mc + 1) * 128], yag[:, kc, :], kc == 0, kc == 3, [tk_w4, yagk], [bak], kc == 3)
                        t1, t1k = t1r.next()
                        t2, t2k = t2r.next()
                        S.op("dve", lambda e: e.tensor_tensor(t1[:], bh[:, :], sgg[:, mc, :], op=ALU.mult), reads=[bhk, sggk], writes=[t1k])
                        S.op("dve", lambda e: e.tensor_tensor(t2[:], ba[:, :], sgg[:, 8 + mc, :], op=ALU.mult), reads=[bak, sggk], writes=[t2k])
                        S.op("pool", lambda e: e.tensor_tensor(mT[:, mc, :], t1[:], t2[:], op=ALU.add), reads=[t1k, t2k], writes=[mTk])
                    for m in range(4):
                        t0 = g * 512 + m * 128
                        xt, xtk = xts[m]
                        x1, x1k = x1r.next()
                        for half in range(2):
                            bank, bk = psum()
                            for kc in range(8):
                                mm(bank[:, :], mT[:, kc, m * 128:(m + 1) * 128], wo_[:, kc, half * 512:(half + 1) * 512], kc == 0, kc == 7,
                                   [mTk, tk_w4], [bk], kc == 7)
                            S.op("dve", lambda e: e.tensor_tensor(x1[:, half * 512:(half + 1) * 512], bank[:, :],
                                                                  xt[:, half * 512:(half + 1) * 512], op=ALU.add),
                                 reads=[bk, xtk], writes=[x1k])
                        S.dma("sp", X1[s * L + t0:s * L + t0 + 128, :], x1[:], reads=[x1k], writes=[tk_x1])
            S.barrier()

    with ExitStack() as st:
        wd = sb(st, "wd", [128, NFF, D], BF16)
        g2 = sb(st, "g2", [128, D], F32)
        gf = sb(st, "gf", [128, D], F32)
        tk_w5 = Tk()
        S.dma("pool", wd[:], I["w_down"].rearrange("(k p) c -> p k c", p=128), reads=[tk_in], writes=[tk_w5])
        S.dma("sp", g2[:], I["g2r"][:, :], reads=[tk_in], writes=[tk_w5])
        S.dma("sp", gf[:], I["gfr"][:, :], reads=[tk_in], writes=[tk_w5])
        wgr = Ring(st, "wg", [128, 8, 128], BF16, 4)
        wur = Ring(st, "wu", [128, 8, 128], BF16, 4)
        x1r = Ring(st, "x15", [128, D], F32, 8)
        xnr = Ring(st, "xn5", [128, D], BF16, 4)
        smr = Ring(st, "sm5", [128, 4], F32, 8)
        junk = sb(st, "junk5", [128, D], BF16)
        junk_tk = Tk()
        h2r = Ring(st, "h2T", [128, 8, 512], BF16, 2)
        actr = Ring(st, "actT", [128, NFF, 512], BF16, 1)
        sgr = Ring(st, "sil", [128, 512], BF16, 3)
        xfr = Ring(st, "xf5", [128, D], F32, 2)
        outr = Ring(st, "out5", [128, D], F32, 2)
        wg_v = I["w_gate"].rearrange("(k p) c -> p k c", p=128)
        wu_v = I["w_up"].rearrange("(k p) c -> p k c", p=128)
        def norm5_stats(g):
            h2, h2k = h2r.next()
            srcs = [X1[g * 512 + m * 128:g * 512 + (m + 1) * 128, :] for m in range(4)]
            xtiles, xns = norm_group_to_T(x1r, xnr, smr, srcs, h2, h2k, [m * 128 for m in range(4)], junk, junk_tk, g2, tk_w5,
                                          phase="stats")
            return h2, h2k, xtiles, xns

        def norm5_trans(st5):
            h2, h2k, xtiles, xns = st5
            norm_trans(xns, h2, h2k, [m * 128 for m in range(4)])

        nxt5 = norm5_stats(0)
        norm5_trans(nxt5)
        for g in range(NSEQ * 8):
            h2, h2k, xtiles, _ = nxt5
            act_, actk = actr.next()
            for j in range(NFF):
                wg, wgk = wgr.next()
                wu, wuk = wur.next()
                S.dma("pool", wg[:], wg_v[:, :, j * 128:(j + 1) * 128], reads=[tk_in], writes=[wgk])
                S.dma("pool", wu[:], wu_v[:, :, j * 128:(j + 1) * 128], reads=[tk_in], writes=[wuk])
                bg, bgk = psum()
                bu, buk = psum()
                for kc in range(8):
                    mm(bg[:, :], wg[:, kc, :], h2[:, kc, :], kc == 0, kc == 7, [wgk, h2k], [bgk], kc == 7)
                for kc in range(8):
                    mm(bu[:, :], wu[:, kc, :], h2[:, kc, :], kc == 0, kc == 7, [wuk, h2k], [buk], kc == 7)
                sl, slk = sgr.next()
                S.op("act", lambda e: e.activation(out=sl[:], in_=bg[:, :], func=AF.Silu), reads=[bgk], writes=[slk])
                S.op("dve", lambda e: e.tensor_tensor(act_[:, j, :], bu[:, :], sl[:], op=ALU.mult), reads=[buk, slk], writes=[actk])
            if g + 1 < NSEQ * 8:
                nxt5 = norm5_stats(g + 1)
            for m in range(4):
                if m == 2 and g + 1 < NSEQ * 8:
                    norm5_trans(nxt5)
                t0 = g * 512 + m * 128
                xt, xtk = xtiles[m]
                xf, xfk = xfr.next()
                for half in range(2):
                    bank, bk = psum()
                    for j in range(NFF):
                        mm(bank[:, :], act_[:, j, m * 128:(m + 1) * 128], wd[:, j, half * 512:(half + 1) * 512], j == 0, j == NFF - 1,
                           [actk, tk_w5], [bk], j == NFF - 1)
                    S.op("dve", lambda e: e.tensor_tensor(xf[:, half * 512:(half + 1) * 512], bank[:, :],
                                                          xt[:, half * 512:(half + 1) * 512], op=ALU.add),
                         reads=[bk, xtk], writes=[xfk])
                sm, smk = smr.next()
                S.op("act", lambda e: e.activation(out=junk[:], in_=xf[:], func=AF.Square, accum_out=sm[:, 0:1]),
                     reads=[xfk], writes=[junk_tk, smk])
                S.op("dve", lambda e: e.tensor_scalar(sm[:, 1:2], sm[:, 0:1], 1.0 / D, EPS, op0=ALU.mult, op1=ALU.add),
                     reads=[smk], writes=[smk])
                S.op("act", lambda e: e.activation(out=sm[:, 2:3], in_=sm[:, 1:2], func=AF.Sqrt), reads=[smk], writes=[smk])
                S.op("dve", lambda e: e.reciprocal(sm[:, 3:4], sm[:, 2:3]), reads=[smk], writes=[smk])
                ot, otk = outr.next()
                S.op("dve", lambda e: e.scalar_tensor_tensor(out=ot[:], in0=xf[:], scalar=sm[:, 3:4], in1=gf[:], op0=ALU.mult,
                                                             op1=ALU.mult), reads=[xfk, smk, tk_w5], writes=[otk])
                S.dma("sp", yout[t0 // L, t0 % L:t0 % L + 128, :], ot[:], reads=[otk], writes=[tk_out])
    S.barrier(engines=("sp",))
    es.close()
    return nc, S


_CACHE = {}


def _prep_inputs(inp):
    f32 = np.float32
    g = lambda k: np.ascontiguousarray(np.asarray(inp[k], dtype=f32))
    shared = {
        "w_in": g("w_in")[0], "g1r": np.ascontiguousarray(np.broadcast_to(g("norm1_g")[0][None], (128, D))),
        "shortwp": np.ascontiguousarray(g("short_w")[0].reshape(3, 12, 128).transpose(2, 1, 0)),
        "shortbp": np.ascontiguousarray(g("short_b")[0].reshape(12, 128).T),
        "fw0": g("filt_w0")[0], "fb0": np.ascontiguousarray(g("filt_b0")[0].reshape(64, 1)),
        "fwi": g("filt_w_inner")[0], "fbi": np.ascontiguousarray(g("filt_b_inner")[0].T),
        "ffreq": np.ascontiguousarray(g("filt_freq")[0].reshape(64, 1)), "fwo": g("filt_w_out")[0],
        "hbias": np.ascontiguousarray(g("hyena_bias")[0].reshape(1, 1024)),
        "sinkr": np.ascontiguousarray(np.broadcast_to(g("sink_logit")[0][None], (128, 8))),
        "w_up_h": g("w_up_hyena")[0], "w_up_a": g("w_up_attn")[0], "w_o": g("w_out")[0],
        "g2r": np.ascontiguousarray(np.broadcast_to(g("norm2_g")[0][None], (128, D))),
        "w_gate": g("w_ff_gate")[0], "w_up": g("w_ff_up")[0], "w_down": g("w_ff_down")[0],
        "gfr": np.ascontiguousarray(np.broadcast_to(g("final_g")[None], (128, D))),
    }
    shared.update(_consts())
    return shared


def kernel(**inputs):
    xp = np.asarray(inputs["x_prompt"], dtype=np.float32)
    xs = np.asarray(inputs["x_sample"], dtype=np.float32)
    shared = _prep_inputs(inputs)
    if "nc" not in _CACHE:
        _CACHE["nc"] = build()[0]
    nc = _CACHE["nc"]
    in_maps = []
    for c in range(8):
        if c < 4:
            xc = xp[2 * c:2 * c + 2]
        else:
            xc = np.stack([xs[c - 4], xs[c - 4]], 0)
        m = dict(shared)
        m["x"] = np.ascontiguousarray(xc)
        in_maps.append(m)
    res = run_bass_kernel_spmd(nc, in_maps, core_ids=list(range(8)))
    yp = np.concatenate([res.results[c]["y"] for c in range(4)], 0).astype(np.float32)
    ys = np.stack([res.results[c]["y"][0] for c in range(4, 8)], 0).astype(np.float32)
    return (yp, ys)
```
